# Optimizing a Trainium2 kernel written in Bass

```python
import jax, jax.numpy as jnp
from jax import lax
import numpy as np

D_MODEL = 1024
BATCH = 2
SEQ = 8192
DEPTH = 1

MLA_HEADS = 8
MLA_NOPE = 128
MLA_ROPE = 64
MLA_V = 128
Q_LORA = 384
KV_LORA = 256
MOBA_HEADS = 8
MOBA_HD = 128
MOBA_BLOCK = 256
MOBA_TOPK = 3
MOBA_QCHUNK = 32
Q_BLOCK = 128
D_FF = 2816
CONV_W = 3

ROPE_THETA = 10000.0
EPS = 1e-6
NEG = -1e30
N_BRANCH = 2

OFF_QLAT = 0
OFF_KVLAT = OFF_QLAT + Q_LORA
OFF_KPE = OFF_KVLAT + KV_LORA
OFF_MOBA = OFF_KPE + MLA_ROPE
OFF_GATE = OFF_MOBA + 3 * MOBA_HEADS * MOBA_HD
IN_COLS = OFF_GATE + N_BRANCH * D_MODEL

kernel_name = 'hybrid_mla_moba_convffn_block'

F32 = jnp.float32


def rmsnorm(x, g):
    xf = x.astype(F32)
    y = xf * lax.rsqrt(jnp.mean(xf * xf, axis=-1, keepdims=True) + EPS)
    return (y * g.astype(F32)).astype(x.dtype)


def rope(x, pos):
    d = x.shape[-1]
    inv = ROPE_THETA ** (-jnp.arange(0, d, 2, dtype=F32) / d)
    ang = pos.astype(F32)[:, None] * inv[None, :]
    cos, sin = jnp.cos(ang), jnp.sin(ang)
    x1, x2 = jnp.split(x.astype(F32), 2, axis=-1)
    out = jnp.concatenate([x1 * cos - x2 * sin, x1 * sin + x2 * cos], axis=-1)
    return out.astype(x.dtype)


def causal_attention_blocks(q, k, v, scale):
    B, H, S, Dk = q.shape
    nq = S // Q_BLOCK
    qb = jnp.moveaxis(q.reshape(B, H, nq, Q_BLOCK, Dk), 2, 0)
    kpos = jnp.arange(S)

    def blk_fn(args):
        i, qi = args
        qpos = i * Q_BLOCK + jnp.arange(Q_BLOCK)
        s = jnp.einsum('bhqd,bhkd->bhqk', qi, k).astype(F32) * scale
        s = jnp.where(kpos[None, :] <= qpos[:, None], s, NEG)
        p = jax.nn.softmax(s, axis=-1).astype(v.dtype)
        return jnp.einsum('bhqk,bhkd->bhqd', p, v)

    out = lax.map(blk_fn, (jnp.arange(nq), qb))
    return jnp.moveaxis(out, 0, 2).reshape(B, H, S, v.shape[-1])


def moba_attention(q, k, v):
    B, H, S, D = q.shape
    nb = -(-S // MOBA_BLOCK)
    Sp = nb * MOBA_BLOCK
    pad = ((0, 0), (0, 0), (0, Sp - S), (0, 0))
    q, k, v = jnp.pad(q, pad), jnp.pad(k, pad), jnp.pad(v, pad)
    kb = k.reshape(B, H, nb, MOBA_BLOCK, D)
    vb = v.reshape(B, H, nb, MOBA_BLOCK, D)
    kmean = jnp.mean(kb.astype(F32), axis=3)
    gate = jnp.einsum('bhsd,bhnd->bhsn', q.astype(F32), kmean)
    q_blk = jnp.arange(Sp) // MOBA_BLOCK
    past = jnp.arange(nb)[None, :] < q_blk[:, None]
    gate = jnp.where(past, gate, -jnp.inf)
    k_sel = min(MOBA_TOPK, nb)
    top_s, top_idx = lax.top_k(gate, k_sel)
    top_ok = jnp.isfinite(top_s)

    nc = Sp // MOBA_QCHUNK

    def to_chunks(t):
        return jnp.moveaxis(t.reshape((B, H, nc, MOBA_QCHUNK) + t.shape[3:]), 2, 0)

    scale = D ** -0.5
    gather = jax.vmap(jax.vmap(lambda t, ix: t[ix]))

    def chunk_fn(args):
        c, qc, ic, okc = args
        qpos = c * MOBA_QCHUNK + jnp.arange(MOBA_QCHUNK)
        blk = (c * MOBA_QCHUNK) // MOBA_BLOCK
        ks = gather(kb, ic)
        vs = gather(vb, ic)
        ko = lax.dynamic_index_in_dim(kb, blk, axis=2, keepdims=False)
        vo = lax.dynamic_index_in_dim(vb, blk, axis=2, keepdims=False)
        s_sel = jnp.einsum('bhcd,bhckjd->bhckj', qc, ks).astype(F32) * scale
        s_sel = jnp.where(okc[..., None], s_sel, NEG)
        s_own = jnp.einsum('bhcd,bhjd->bhcj', qc, ko).astype(F32) * scale
        kpos = blk * MOBA_BLOCK + jnp.arange(MOBA_BLOCK)
        s_own = jnp.where(kpos[None, :] <= qpos[:, None], s_own, NEG)
        s = jnp.concatenate([s_sel.reshape(B, H, MOBA_QCHUNK, k_sel * MOBA_BLOCK), s_own], axis=-1)
        p = jax.nn.softmax(s, axis=-1).astype(v.dtype)
        p_sel = p[..., :k_sel * MOBA_BLOCK].reshape(B, H, MOBA_QCHUNK, k_sel, MOBA_BLOCK)
        p_own = p[..., k_sel * MOBA_BLOCK:]
        return (jnp.einsum('bhckj,bhckjd->bhcd', p_sel, vs)
                + jnp.einsum('bhcj,bhjd->bhcd', p_own, vo))

    out = lax.map(chunk_fn, (jnp.arange(nc), to_chunks(q), to_chunks(top_idx), to_chunks(top_ok)))
    return jnp.moveaxis(out, 0, 2).reshape(B, H, Sp, D)[:, :, :S]


def causal_dwconv(u, w, b):
    S = u.shape[1]
    up = jnp.pad(u, ((0, 0), (CONV_W - 1, 0), (0, 0)))
    y = b
    for j in range(CONV_W):
        y = y + w[j] * up[:, j:j + S]
    return y


def setup_inputs(seed: int = 0) -> dict:
    key = jax.random.key(seed)
    ks = jax.random.split(key, 20)
    L = DEPTH

    def w(k, shape, fan_in):
        return jax.random.normal(k, shape, F32) * fan_in ** -0.5

    def gain(k, shape):
        return 1.0 + 0.02 * jax.random.normal(k, shape, F32)

    return {
        'x': jax.random.normal(ks[0], (BATCH, SEQ, D_MODEL), F32),
        'attn_norm': gain(ks[1], (L, D_MODEL)),
        'w_in': w(ks[2], (L, D_MODEL, IN_COLS), D_MODEL),
        'b_gate': 0.02 * jax.random.normal(ks[3], (L, N_BRANCH * D_MODEL), F32),
        'q_norm': gain(ks[4], (L, Q_LORA)),
        'w_uq': w(ks[5], (L, Q_LORA, MLA_HEADS * (MLA_NOPE + MLA_ROPE)), Q_LORA),
        'kv_norm': gain(ks[6], (L, KV_LORA)),
        'w_ukv': w(ks[7], (L, KV_LORA, MLA_HEADS * (MLA_NOPE + MLA_V)), KV_LORA),
        'w_o_mla': w(ks[8], (L, MLA_HEADS * MLA_V, D_MODEL), MLA_HEADS * MLA_V),
        'w_o_moba': w(ks[9], (L, MOBA_HEADS * MOBA_HD, D_MODEL), MOBA_HEADS * MOBA_HD),
        'w_out': w(ks[10], (L, D_MODEL, D_MODEL), D_MODEL),
        'ffn_norm': gain(ks[11], (L, D_MODEL)),
        'w_up': w(ks[12], (L, D_MODEL, 2 * D_FF), D_MODEL),
        'conv_w': w(ks[13], (L, CONV_W, 2 * D_FF), CONV_W),
        'conv_b': 0.02 * jax.random.normal(ks[14], (L, 2 * D_FF), F32),
        'w_down': w(ks[15], (L, D_FF, D_MODEL), D_FF),
        'final_norm': gain(ks[16], (D_MODEL,)),
    }


def reference(x, attn_norm, w_in, b_gate, q_norm, w_uq, kv_norm, w_ukv, w_o_mla, w_o_moba,
              w_out, ffn_norm, w_up, conv_w, conv_b, w_down, final_norm):
    B, S, _ = x.shape
    pos = jnp.arange(S)
    for l in range(DEPTH):
        h = rmsnorm(x, attn_norm[l])
        proj = h @ w_in[l]
        q_lat = proj[..., OFF_QLAT:OFF_KVLAT]
        kv_lat = proj[..., OFF_KVLAT:OFF_KPE]
        k_pe = proj[..., OFF_KPE:OFF_MOBA]
        qkv_b = proj[..., OFF_MOBA:OFF_GATE]
        gates = jax.nn.sigmoid(proj[..., OFF_GATE:] + b_gate[l])
        g_a, g_b = gates[..., :D_MODEL], gates[..., D_MODEL:]

        q = (rmsnorm(q_lat, q_norm[l]) @ w_uq[l]).reshape(B, S, MLA_HEADS, MLA_NOPE + MLA_ROPE)
        q = q.transpose(0, 2, 1, 3)
        kv = (rmsnorm(kv_lat, kv_norm[l]) @ w_ukv[l]).reshape(B, S, MLA_HEADS, MLA_NOPE + MLA_V)
        kv = kv.transpose(0, 2, 1, 3)
        q_a = jnp.concatenate([q[..., :MLA_NOPE], rope(q[..., MLA_NOPE:], pos)], axis=-1)
        k_rot = jnp.broadcast_to(rope(k_pe[:, None], pos), (B, MLA_HEADS, S, MLA_ROPE))
        k_a = jnp.concatenate([kv[..., :MLA_NOPE], k_rot], axis=-1)
        v_a = kv[..., MLA_NOPE:]
        y_a = causal_attention_blocks(q_a, k_a, v_a, (MLA_NOPE + MLA_ROPE) ** -0.5)
        y_a = y_a.transpose(0, 2, 1, 3).reshape(B, S, MLA_HEADS * MLA_V) @ w_o_mla[l]

        qkv = qkv_b.reshape(B, S, 3, MOBA_HEADS, MOBA_HD).transpose(2, 0, 3, 1, 4)
        y_b = moba_attention(rope(qkv[0], pos), rope(qkv[1], pos), qkv[2])
        y_b = y_b.transpose(0, 2, 1, 3).reshape(B, S, MOBA_HEADS * MOBA_HD) @ w_o_moba[l]

        x = x + (g_a * y_a + g_b * y_b) @ w_out[l]

        h = rmsnorm(x, ffn_norm[l])
        u = causal_dwconv(h @ w_up[l], conv_w[l], conv_b[l])
        x = x + (jax.nn.silu(u[..., :D_FF]) * u[..., D_FF:]) @ w_down[l]
    return rmsnorm(x, final_norm)
```

```python
import contextlib
import numpy as np
import ml_dtypes
import concourse.bass as bass
import concourse.mybir as mybir
from concourse.bass_utils import run_bass_kernel_spmd

F32 = mybir.dt.float32
BF16 = mybir.dt.bfloat16
AF = mybir.ActivationFunctionType
ALU = mybir.AluOpType
AX = mybir.AxisListType

D = 1024
EPS = 1e-6
DFF = 2816
NPAIR = DFF // 128
MBIG = 30000.0


ENGS = ("pe", "act", "dve", "pool", "sp")
N_DMA_SEMS = 24


class Prog:
    def __init__(self, nc, same_engine_sync=True):
        self.nc = nc
        self.ops = {e: [] for e in ENGS}
        self.last_w = {}
        self.readers = {}
        self.same_engine_sync = same_engine_sync
        self.n_dma = 0
        self.dmas_since_bar = []
        self.bar_deps = {}

    def op(self, eng, fn, r=(), w=(), dma=False, inc=16):
        idx = len(self.ops[eng])
        deps = {}

        def add(d):
            if d is None:
                return
            e, i = d
            if e == eng and not dma and self.ops[e][i]["dma"] is None:
                if eng == "pe" or not self.same_engine_sync:
                    return
            if e == eng and i == idx:
                return
            key = (e, i) if self.ops[e][i]["dma"] is not None else e
            if key == e:
                deps[e] = max(deps.get(e, -1), i)
            else:
                deps[key] = i

        for t in r:
            add(self.last_w.get(t))
        for t in w:
            add(self.last_w.get(t))
            for rd in self.readers.get(t, ()):
                add(rd)
        for (e2, i2) in self.bar_deps.pop(eng, ()):
            if self.ops[e2][i2]["dma"] is not None:
                deps[(e2, i2)] = i2
            elif e2 != eng or eng != "pe":
                deps[e2] = max(deps.get(e2, -1), i2)
        rec = dict(fn=fn, deps=deps, dma=(self.n_dma if dma else None), sig=False, inc=inc)
        if dma:
            self.n_dma += 1
        self.ops[eng].append(rec)
        me = (eng, idx)
        if dma:
            self.dmas_since_bar.append(me)
        for t in r:
            self.readers.setdefault(t, []).append(me)
        for t in w:
            self.last_w[t] = me
            self.readers[t] = []
        return me

    def barrier(self):
        deps = list(self.dmas_since_bar)
        for e in ENGS:
            for i in range(len(self.ops[e]) - 1, -1, -1):
                if self.ops[e][i]["dma"] is None:
                    deps.append((e, i))
                    break
        for e in ENGS:
            self.bar_deps[e] = list(self.bar_deps.get(e, [])) + deps
        self.dmas_since_bar = []

    def emit(self, final_wait=()):
        nc = self.nc
        for e in ENGS:
            for rec in self.ops[e]:
                for k, i in rec["deps"].items():
                    pe = k if isinstance(k, str) else k[0]
                    self.ops[pe][i]["sig"] = True
        for (e, i) in final_wait:
            self.ops[e][i]["sig"] = True
        import contextlib
        stack = contextlib.ExitStack()
        sems = {e: stack.enter_context(nc.semaphore("s_" + e)) for e in ENGS}
        dsems = [stack.enter_context(nc.semaphore("d%d" % i)) for i in range(N_DMA_SEMS)]
        dcount = [0] * N_DMA_SEMS
        for e in ENGS:
            c = 0
            for rec in self.ops[e]:
                if rec["dma"] is not None:
                    continue
                if rec["sig"]:
                    c += 1
                    rec["val"] = c
        dma_recs = []
        for e in ENGS:
            for rec in self.ops[e]:
                if rec["dma"] is not None:
                    dma_recs.append(rec)
        dma_recs.sort(key=lambda r: r["dma"])
        for rec in dma_recs:
            s = rec["dma"] % N_DMA_SEMS
            rec["dsem"] = s
            rec["prev"] = dcount[s]
            dcount[s] += rec["inc"]
            rec["val"] = dcount[s]

        handles = {"pe": nc.tensor, "act": nc.scalar, "dve": nc.vector, "pool": nc.gpsimd, "sp": nc.sync}
        block = stack.enter_context(nc.Block())

        def run(e, h):
            known = {}
            knownd = {}
            for rec in self.ops[e]:
                for k, i in rec["deps"].items():
                    if isinstance(k, str):
                        v = self.ops[k][i]["val"]
                        if known.get(k, 0) >= v:
                            continue
                        known[k] = v
                        h.wait_ge(sems[k], v)
                    else:
                        d = self.ops[k[0]][i]
                        if knownd.get(d["dsem"], 0) < d["val"]:
                            knownd[d["dsem"]] = d["val"]
                            h.wait_ge(dsems[d["dsem"]], d["val"])
                if rec["dma"] is not None and rec["prev"] > knownd.get(rec["dsem"], 0):
                    knownd[rec["dsem"]] = rec["prev"]
                    h.wait_ge(dsems[rec["dsem"]], rec["prev"])
                ins = rec["fn"](h)
                if rec["dma"] is not None:
                    ins.then_inc(dsems[rec["dsem"]], rec["inc"])
                elif rec["sig"]:
                    ins.then_inc(sems[e], 1)
            if e == "sp":
                for (fe, fi) in final_wait:
                    d = self.ops[fe][fi]
                    if d["dma"] is not None:
                        h.wait_ge(dsems[d["dsem"]], d["val"])
                    else:
                        h.wait_ge(sems[fe], d["val"])

        block.tensor(lambda h: run("pe", h))
        block.scalar(lambda h: run("act", h))
        block.vector(lambda h: run("dve", h))
        block.gpsimd(lambda h: run("pool", h))
        block.sync(lambda h: run("sp", h))
        stack.close()


class Ring:
    def __init__(self, alloc, name, shape, dtype, n):
        self.t = [alloc(f"{name}{i}", shape, dtype) for i in range(n)]
        self.i = 0
        self.name = name
        self.n = n

    def next(self):
        k = self.i % self.n
        self.i += 1
        return self.t[k], (self.name, k)


def build(S, debug=False, stop=0):
    import os
    stop = int(os.environ.get("KSTOP", stop))
    NG = S // 512
    NT = S // 128
    OWN = S // 4
    HALF = OWN // 2
    nc = bass.Bass("TRN2", target_bir_lowering=False)
    P = Prog(nc)

    def din(name, shape, dt=F32):
        return nc.dram_tensor(name, list(shape), dt, kind="ExternalInput").ap()

    def dscr(name, shape, dt=BF16):
        if debug:
            return nc.dram_tensor(name, list(shape), dt, kind="ExternalOutput").ap()
        return nc.dram_tensor(name, list(shape), dt).ap()

    xb = din("xb", [S, D])
    xown = din("xown", [OWN + 2, D])
    flag_d = din("flag", [128, 1])
    gattn_d = din("gattn", [128, D])
    glat_d = din("glat", [128, 640])
    gffn_d = din("gffn", [128, D])
    gfin_d = din("gfin", [128, D])
    w_lat_d = din("w_lat", [128, 8 * 640])
    w_kpe_d = din("w_kpe", [128, 8 * 256])
    w_mq_d = din("w_mq", [128, 8 * 512])
    w_mk_d = din("w_mk", [128, 8 * 512])
    w_mv_d = din("w_mv", [128, 8 * 256])
    w_uqn_d = din("w_uqn", [128, 3 * 256])
    w_uqr_d = din("w_uqr", [128, 3 * 256])
    w_ukk_d = din("w_ukk", [128, 2 * 256])
    w_ukv_d = din("w_ukv", [128, 2 * 256])
    w_gate_d = din("w_gate", [128, 8 * 2048])
    bgate_d = din("bgate", [128, 16])
    w_oa_d = din("w_oa", [128, 8 * 1024])
    w_ob_d = din("w_ob", [128, 8 * 1024])
    w_out_d = din("w_out", [128, 8 * 1024])
    w_up_d = din("w_up", [NPAIR, 128, 2 * 8 * 128])
    convw_d = din("convw", [128, 2 * NPAIR * 3])
    convb_d = din("convb", [128, 2 * NPAIR])
    w_down_d = din("w_down", [128, NPAIR * 1024])
    cos128_d = din("cos128", [128, S])
    sin128_d = din("sin128", [128, S])
    cos64_d = din("cos64", [128, S])
    sin64_d = din("sin64", [128, S])
    ident_d = din("ident", [128, 128], BF16)
    cm_d = din("cm", [128, 4 * 512], BF16)
    oh_d = din("oh", [32, 32 * 128], BF16)
    out_d = nc.dram_tensor("out", [OWN, D], F32, kind="ExternalOutput").ap()

    aqn_d = dscr("aqn", [2, 128, S])
    aqr_d = dscr("aqr", [128, S])
    akn_d = dscr("akn", [2, 128, S])
    kpe_d = dscr("kpe", [128, S])
    av_d = dscr("av", [128, NT * 2 * 128])
    mq_d = dscr("mq", [2, 128, S])
    mk_d = dscr("mk", [2, 128, S])
    mv_d = dscr("mv", [128, NT * 2 * 128])
    mb_d = dscr("mb", [2, 32, S])
    CW = min(1024, OWN)
    NCH = S // CW
    cc_in = nc.dram_tensor("cc_in", [NCH, 512, CW], BF16).ap()
    cc_out = nc.dram_tensor("cc_out", [NCH + 1, 2048, CW], BF16).ap()
    oall_d = dscr("oall", [512, S]) if debug else None
    x1d = dscr("x1d", [OWN + 2, D], F32)

    def alloc(name, shape, dt):
        return nc.alloc_sbuf_tensor(name, list(shape), dt)

    ident = alloc("ident_s", [128, 128], BF16)
    ones = alloc("ones_s", [128, 128], BF16)
    cm = alloc("cm_s", [128, 4, 512], BF16)
    oh = alloc("oh_s", [32, 32 * 128], BF16)
    PB = alloc("PB_s", [128, 64], F32)
    OW = alloc("OW_s", [128, 64], F32)
    flag = alloc("flag_s", [128, 1], F32)
    zpad = alloc("zpad_s", [128, 16, 2], BF16)
    P.op("sp", lambda e: e.dma_start(out=ident[:], in_=ident_d), w=["ident"], dma=True)
    P.op("sp", lambda e: e.dma_start(out=cm[:], in_=cm_d.rearrange("p (j q) -> p j q", j=4)), w=["cm"], dma=True)
    P.op("sp", lambda e: e.dma_start(out=oh[:], in_=oh_d), w=["oh"], dma=True)
    P.op("sp", lambda e: e.dma_start(out=flag[:], in_=flag_d), w=["flag"], dma=True)
    P.op("pool", lambda e: e.memset(ones[:], 1.0), w=["ones"])
    P.op("pool", lambda e: e.memset(PB[:, 0:32], 0.0), w=["PB"])
    P.op("pool", lambda e: e.memset(PB[:, 32:64], -1e30), w=["PB"])
    P.op("pool", lambda e: e.memset(OW[:], 0.0), w=["OW"])
    P.op("pool", lambda e: e.memset(OW[:, 32:33], 1.0), w=["OW"])
    P.op("pool", lambda e: e.memset(zpad[:], 0.0), w=["zpad"])
    P.op("pool", lambda e: e.dma_start(out=cc_out[0, :, CW - 2:CW].rearrange("(b p) n -> p b n", p=128), in_=zpad[:]),
         r=["zpad"], dma=True)

    banks = [nc.alloc_psum_tensor(f"ps{i}", [128, 512], F32) for i in range(8)]
    banks_bf = [b.bitcast(BF16) for b in banks]

    class PsPool:
        def __init__(self, idxs):
            self.idxs = idxs
            self.i = 0

        def next(self):
            k = self.idxs[self.i % len(self.idxs)]
            self.i += 1
            return k, ("ps", k)

    ev_i = [0]

    def evac(out_ap, in_ap, r, w, eng=None):
        if eng is None:
            eng = ("act", "dve")[ev_i[0] % 2]
            ev_i[0] += 1
        if eng == "act":
            P.op("act", lambda e: e.copy(out=out_ap, in_=in_ap), r=r, w=w)
        else:
            P.op("dve", lambda e: e.tensor_copy(out=out_ap, in_=in_ap), r=r, w=w)

    def mm(out_ap, lhsT, rhs, start, stop, r, w):
        P.op("pe", lambda e: e.matmul(out_ap, lhsT=lhsT, rhs=rhs, start=start, stop=stop), r=r, w=w)

    def wload(dst, src2d, tok):
        shp = list(dst.shape)
        if len(shp) == 3:
            src = src2d.rearrange("p (c n) -> p c n", c=shp[1])
        elif len(shp) == 4:
            src = src2d.rearrange("p (a c n) -> p a c n", a=shp[1], c=shp[2])
        else:
            src = src2d
        P.op("pool", lambda e: e.dma_start(out=dst[:], in_=src), w=[tok], dma=True)

    def make_norm_T(alloc_f, pT_pool, nx):
        xring = Ring(alloc_f, "xt", [128, D], F32, nx)
        jring = Ring(alloc_f, "junk", [128, D], BF16, 2)
        hring = Ring(alloc_f, "hrow", [128, D], BF16, 2)
        sring = Ring(alloc_f, "ssn", [128, 4], F32, 4)

        def norm_T(src_rows, m, gbc, gtok, dst3, dst_tok):
            xt, xtok = xring.next()
            P.op("sp", lambda e: e.dma_start(out=xt[0:m, :], in_=src_rows), w=[xtok], dma=True)
            jk, jtok = jring.next()
            ss, stok = sring.next()
            P.op("act", lambda e: e.activation(out=jk[0:m, :], in_=xt[0:m, :], func=AF.Square, accum_out=ss[0:m, 0:1]),
                 r=[xtok], w=[jtok, stok])
            P.op("dve", lambda e: e.tensor_scalar(out=ss[0:m, 1:2], in0=ss[0:m, 0:1], scalar1=1.0 / D, scalar2=EPS,
                                                  op0=ALU.mult, op1=ALU.add), r=[stok], w=[stok])
            P.op("act", lambda e: e.activation(out=ss[0:m, 2:3], in_=ss[0:m, 1:2], func=AF.Sqrt), r=[stok], w=[stok])
            P.op("dve", lambda e: e.reciprocal(out=ss[0:m, 3:4], in_=ss[0:m, 2:3]), r=[stok], w=[stok])
            hr, htok = hring.next()
            P.op("dve", lambda e: e.scalar_tensor_tensor(out=hr[0:m, :], in0=xt[0:m, :], scalar=ss[0:m, 3:4],
                                                         in1=gbc[0:m, :], op0=ALU.mult, op1=ALU.mult),
                 r=[xtok, stok, gtok], w=[htok])
            k, ptok = pT_pool.next()
            pT = banks_bf[k]
            for c in range(8):
                P.op("pe", lambda e, c=c: e.transpose(out=pT[:, c * 128:c * 128 + m], in_=hr[0:m, c * 128:(c + 1) * 128],
                                                      identity=ident[0:m, 0:m]), r=[htok, "ident"], w=[ptok])
            evac(dst3, pT[:, :].rearrange("p (c t) -> p c t", c=8)[:, :, 0:m], r=[ptok], w=[dst_tok])
            return xt, xtok, ss, stok

        norm_T.jring = jring
        return norm_T

    with contextlib.ExitStack() as st1:
        def a1(name, shape, dt):
            return st1.enter_context(nc.sbuf_tensor("p1_" + name, list(shape), dt))

        gattn = a1("gattn", [128, D], F32)
        glat = a1("glat", [128, 640], F32)
        P.op("sp", lambda e: e.dma_start(out=gattn[:], in_=gattn_d), w=["gattn"], dma=True)
        P.op("sp", lambda e: e.dma_start(out=glat[:], in_=glat_d), w=["glat"], dma=True)
        w_lat = a1("w_lat", [128, 8, 640], BF16)
        w_kpe = a1("w_kpe", [128, 8, 256], BF16)
        w_mq = a1("w_mq", [128, 8, 512], BF16)
        w_mk = a1("w_mk", [128, 8, 512], BF16)
        w_mv = a1("w_mv", [128, 8, 256], BF16)
        w_uqn = a1("w_uqn", [128, 3, 256], BF16)
        w_uqr = a1("w_uqr", [128, 3, 256], BF16)
        w_ukk = a1("w_ukk", [128, 2, 256], BF16)
        w_ukv = a1("w_ukv", [128, 2, 256], BF16)
        for t_, d_, nm in ((w_lat, w_lat_d, "w_lat"), (w_kpe, w_kpe_d, "w_kpe"), (w_mk, w_mk_d, "w_mk"),
                           (w_mq, w_mq_d, "w_mq"), (w_mv, w_mv_d, "w_mv"), (w_uqn, w_uqn_d, "w_uqn"),
                           (w_uqr, w_uqr_d, "w_uqr"), (w_ukk, w_ukk_d, "w_ukk"), (w_ukv, w_ukv_d, "w_ukv")):
            wload(t_, d_, nm)
        ksum = a1("ksum", [128, 2, 32], F32)
        P.op("pool", lambda e: e.memset(ksum[:], 0.0), w=["ksum0", "ksum1"])

        ptp = PsPool([6, 7])
        psp = PsPool([0, 1, 2, 3, 4, 5])
        norm_T = make_norm_T(a1, ptp, 3)
        hT_ring = Ring(a1, "hT", [128, 8, 512], BF16, 2)
        latn_ring = Ring(a1, "latn", [128, 640], BF16, 2)
        latT_ring = Ring(a1, "latT", [128, 5, 512], BF16, 2)
        ss2_ring = Ring(a1, "ss2", [128, 8], F32, 4)
        tab_ring = Ring(a1, "tab", [128, 4, 512], F32, 2)
        t1_ring = Ring(a1, "rt1", [128, 512], F32, 2)
        t2_ring = Ring(a1, "rt2", [128, 512], F32, 2)
        of_ring = Ring(a1, "rof", [128, 512], F32, 3)
        ob_ring = Ring(a1, "rob", [128, 512], BF16, 4)
        vst_ring = Ring(a1, "vst", [128, 4, 256], BF16, 2)
        gm_ring = Ring(a1, "gm", [128, 48], F32, 4)
        mbt_ring = Ring(a1, "mbt", [128, 4, 32], BF16, 2)
        mbs_ring = Ring(a1, "mbs", [32, 512], BF16, 2)
        junk2 = Ring(a1, "junk2", [128, 384], BF16, 2)

        def rope(pa, patok, pb, pbtok, tab, ttok, ci, si, out_ap, out_tok):
            t1, t1tok = t1_ring.next()
            t2, t2tok = t2_ring.next()
            P.op("dve", lambda e: e.tensor_tensor(out=t1[:], in0=banks[pa][:], in1=tab[:, ci, :], op=ALU.mult),
                 r=[patok, ttok], w=[t1tok])
            P.op("dve", lambda e: e.tensor_tensor(out=t2[:], in0=banks[pb][:], in1=tab[:, si, :], op=ALU.mult),
                 r=[pbtok, ttok], w=[t2tok])
            P.op("pool", lambda e: e.tensor_tensor(out=out_ap, in0=t1[:], in1=t2[:], op=ALU.add),
                 r=[t1tok, t2tok], w=[out_tok])

        def proj_pair(wt, wtok, ca, cb, rhsT, rtok, nk, koff=0):
            pa, patok = psp.next()
            for k in range(nk):
                mm(banks[pa][:], wt[:, k, ca:ca + 128], rhsT[:, koff + k, :], k == 0, k == nk - 1, [wtok, rtok], [patok])
            pb, pbtok = psp.next()
            for k in range(nk):
                mm(banks[pb][:], wt[:, k, cb:cb + 128], rhsT[:, koff + k, :], k == 0, k == nk - 1, [wtok, rtok], [pbtok])
            return pa, patok, pb, pbtok

        def proj_one(wt, wtok, ca, rhsT, rtok, nk, koff=0):
            pa, patok = psp.next()
            for k in range(nk):
                mm(banks[pa][:], wt[:, k, ca:ca + 128], rhsT[:, koff + k, :], k == 0, k == nk - 1, [wtok, rtok], [patok])
            return pa, patok

        def store(dst, src, r):
            P.op("pool", lambda e: e.dma_start(out=dst, in_=src), r=r, dma=True)

        for g in range(NG):
            gc = slice(g * 512, (g + 1) * 512)
            hT, hTtok = hT_ring.next()
            tab, ttok = tab_ring.next()
            for i_, td in enumerate((cos128_d, sin128_d, cos64_d, sin64_d)):
                P.op("sp", lambda e, i_=i_, td=td, tab=tab, gc=gc: e.dma_start(out=tab[:, i_, :], in_=td[:, gc]), w=[ttok], dma=True)
            for t in range(4):
                r0 = g * 512 + t * 128
                norm_T(xb[r0:r0 + 128, :], 128, gattn, "gattn", hT[:, :, t * 128:(t + 1) * 128], hTtok)
            latT, latTtok = latT_ring.next()
            for t in range(4):
                tc_ = slice(t * 128, (t + 1) * 128)
                pq, pqtok = psp.next()
                for k in range(8):
                    mm(banks[pq][:, 0:384], hT[:, k, tc_], w_lat[:, k, 0:384], k == 0, k == 7, [hTtok, "w_lat"], [pqtok])
                pk, pktok = psp.next()
                for k in range(8):
                    mm(banks[pk][:, 0:256], hT[:, k, tc_], w_lat[:, k, 384:640], k == 0, k == 7, [hTtok, "w_lat"], [pktok])
                ss, stok = ss2_ring.next()
                j2, j2tok = junk2.next()
                P.op("act", lambda e, pq=pq, j2=j2, ss=ss: e.activation(out=j2[:, 0:384], in_=banks[pq][:, 0:384], func=AF.Square,
                                                                     accum_out=ss[:, 0:1]), r=[pqtok], w=[j2tok, stok])
                P.op("act", lambda e, pk=pk, j2=j2, ss=ss: e.activation(out=j2[:, 0:256], in_=banks[pk][:, 0:256], func=AF.Square,
                                                                     accum_out=ss[:, 1:2]), r=[pktok], w=[j2tok, stok])
                P.op("dve", lambda e, ss=ss: e.tensor_scalar(out=ss[:, 2:3], in0=ss[:, 0:1], scalar1=1.0 / 384, scalar2=EPS,
                                                             op0=ALU.mult, op1=ALU.add), r=[stok], w=[stok])
                P.op("dve", lambda e, ss=ss: e.tensor_scalar(out=ss[:, 3:4], in0=ss[:, 1:2], scalar1=1.0 / 256, scalar2=EPS,
                                                             op0=ALU.mult, op1=ALU.add), r=[stok], w=[stok])
                P.op("act", lambda e, ss=ss: e.activation(out=ss[:, 4:6], in_=ss[:, 2:4], func=AF.Sqrt), r=[stok], w=[stok])
                P.op("dve", lambda e, ss=ss: e.reciprocal(out=ss[:, 6:8], in_=ss[:, 4:6]), r=[stok], w=[stok])
                ln, lntok = latn_ring.next()
                P.op("dve", lambda e, pq=pq, ln=ln, ss=ss: e.scalar_tensor_tensor(
                    out=ln[:, 0:384], in0=banks[pq][:, 0:384], scalar=ss[:, 6:7], in1=glat[:, 0:384],
                    op0=ALU.mult, op1=ALU.mult), r=[pqtok, stok, "glat"], w=[lntok])
                P.op("dve", lambda e, pk=pk, ln=ln, ss=ss: e.scalar_tensor_tensor(
                    out=ln[:, 384:640], in0=banks[pk][:, 0:256], scalar=ss[:, 7:8], in1=glat[:, 384:640],
                    op0=ALU.mult, op1=ALU.mult), r=[pktok, stok, "glat"], w=[lntok])
                kk, ptok = ptp.next()
                pT = banks_bf[kk]
                for c in range(5):
                    P.op("pe", lambda e, c=c, pT=pT, ln=ln: e.transpose(out=pT[:, c * 128:(c + 1) * 128],
                                                                        in_=ln[:, c * 128:(c + 1) * 128], identity=ident[:]),
                         r=[lntok, "ident"], w=[ptok])
                evac(latT[:, :, tc_], pT[:, 0:640].rearrange("p (c t) -> p c t", c=5), r=[ptok], w=[latTtok])
            pa, patok, pb, pbtok = proj_pair(w_kpe, "w_kpe", 0, 128, hT, hTtok, 8)
            ob, obtok = ob_ring.next()
            rope(pa, patok, pb, pbtok, tab, ttok, 2, 3, ob[:], obtok)
            store(kpe_d[:, gc], ob[:], [obtok])
            for j in range(2):
                pa, patok, pb, pbtok = proj_pair(w_mk, "w_mk", j * 128, 256 + j * 128, hT, hTtok, 8)
                of, oftok = of_ring.next()
                rope(pa, patok, pb, pbtok, tab, ttok, 0, 1, of[:], oftok)
                ob, obtok = ob_ring.next()
                P.op("act", lambda e, ob=ob, of=of: e.copy(out=ob[:], in_=of[:]), r=[oftok], w=[obtok])
                store(mk_d[j, :, gc], ob[:], [obtok])
                P.op("dve", lambda e, of=of, j=j, g=g: e.tensor_reduce(out=ksum[:, j, 2 * g:2 * g + 2],
                                                                   in_=of[:].rearrange("p (a b) -> p a b", a=2),
                                                                   axis=AX.X, op=ALU.add), r=[oftok], w=[f"ksum{j}"])
            for j in range(2):
                pa, patok, pb, pbtok = proj_pair(w_mq, "w_mq", j * 128, 256 + j * 128, hT, hTtok, 8)
                of, oftok = of_ring.next()
                rope(pa, patok, pb, pbtok, tab, ttok, 0, 1, of[:], oftok)
                ob, obtok = ob_ring.next()
                P.op("act", lambda e, ob=ob, of=of: e.copy(out=ob[:], in_=of[:]), r=[oftok], w=[obtok])
                store(mq_d[j, :, gc], ob[:], [obtok])
                mbt, mbttok = mbt_ring.next()
                kk, ptok = ptp.next()
                pTb = banks_bf[kk]
                for t in range(4):
                    B = 2 * g + t // 2
                    pg, pgtok = psp.next()
                    mm(banks[pg][:, 0:32], of[:, t * 128:(t + 1) * 128], ksum[:, j, :], True, True, [oftok, f"ksum{j}"], [pgtok])
                    gm, gmtok = gm_ring.next()
                    P.op("dve", lambda e, pg=pg, gm=gm, B=B: e.tensor_tensor(out=gm[:, 0:32], in0=banks[pg][:, 0:32],
                                                                          in1=PB[:, 32 - B:64 - B], op=ALU.add),
                         r=[pgtok, "PB"], w=[gmtok])
                    P.op("dve", lambda e, gm=gm: e.max(out=gm[:, 32:40], in_=gm[:, 0:32]), r=[gmtok], w=[gmtok])
                    P.op("dve", lambda e, gm=gm: e.tensor_scalar(out=gm[:, 40:41], in0=gm[:, 34:35], scalar1=-1e29, scalar2=None,
                                                                 op0=ALU.max), r=[gmtok], w=[gmtok])
                    P.op("dve", lambda e, gm=gm: e.tensor_scalar(out=gm[:, 0:32], in0=gm[:, 0:32], scalar1=gm[:, 40:41],
                                                                 scalar2=None, op0=ALU.is_ge), r=[gmtok], w=[gmtok])
                    P.op("dve", lambda e, gm=gm, B=B: e.tensor_tensor(out=gm[:, 0:32], in0=gm[:, 0:32], in1=OW[:, 32 - B:64 - B],
                                                                      op=ALU.add), r=[gmtok, "OW"], w=[gmtok])
                    P.op("dve", lambda e, gm=gm, mbt=mbt, t=t: e.tensor_scalar(out=mbt[:, t, :], in0=gm[:, 0:32], scalar1=MBIG,
                                                                             scalar2=-MBIG, op0=ALU.mult, op1=ALU.add),
                         r=[gmtok], w=[mbttok])
                    P.op("pe", lambda e, pTb=pTb, mbt=mbt, t=t: e.transpose(out=pTb[0:32, t * 128:(t + 1) * 128], in_=mbt[:, t, :],
                                                                           identity=ident[:]), r=[mbttok, "ident"], w=[ptok])
                mbs, mbstok = mbs_ring.next()
                evac(mbs[:], pTb[0:32, 0:512], r=[ptok], w=[mbstok])
                store(mb_d[j, :, gc], mbs[:], [mbstok])
            vst, vsttok = vst_ring.next()
            for t in range(4):
                tc_ = slice(t * 128, (t + 1) * 128)
                pv, pvtok = psp.next()
                for k in range(8):
                    mm(banks[pv][:, 0:256], hT[:, k, tc_], w_mv[:, k, :], k == 0, k == 7, [hTtok, "w_mv"], [pvtok])
                evac(vst[:, t, :], banks[pv][:, 0:256], r=[pvtok], w=[vsttok])
            store(mv_d[:, g * 1024:(g + 1) * 1024].rearrange("p (t n) -> p t n", t=4), vst[:], [vsttok])
            for j in range(2):
                pa, patok = proj_one(w_uqn, "w_uqn", j * 128, latT, latTtok, 3)
                ob, obtok = ob_ring.next()
                evac(ob[:], banks[pa][:], r=[patok], w=[obtok])
                store(aqn_d[j, :, gc], ob[:], [obtok])
            pa, patok, pb, pbtok = proj_pair(w_uqr, "w_uqr", 0, 128, latT, latTtok, 3)
            ob, obtok = ob_ring.next()
            rope(pa, patok, pb, pbtok, tab, ttok, 2, 3, ob[:], obtok)
            store(aqr_d[:, gc], ob[:], [obtok])
            for j in range(2):
                pa, patok = proj_one(w_ukk, "w_ukk", j * 128, latT, latTtok, 2, koff=3)
                ob, obtok = ob_ring.next()
                evac(ob[:], banks[pa][:], r=[patok], w=[obtok])
                store(akn_d[j, :, gc], ob[:], [obtok])
            vst, vsttok = vst_ring.next()
            for t in range(4):
                tc_ = slice(t * 128, (t + 1) * 128)
                pv, pvtok = psp.next()
                for k in range(2):
                    mm(banks[pv][:, 0:256], latT[:, 3 + k, tc_], w_ukv[:, k, :], k == 0, k == 1, [latTtok, "w_ukv"], [pvtok])
                evac(vst[:, t, :], banks[pv][:, 0:256], r=[pvtok], w=[vsttok])
            store(av_d[:, g * 1024:(g + 1) * 1024].rearrange("p (t n) -> p t n", t=4), vst[:], [vsttok])
    P.barrier()
    if stop == 1:
        P.emit(final_wait=[("pool", len(P.ops["pool"]) - 1)])
        return nc

    for kind in ("mla", "moba"):
        with contextlib.ExitStack() as st2:
            def a2(name, shape, dt):
                return st2.enter_context(nc.sbuf_tensor("p2" + kind + "_" + name, list(shape), dt))

            kT_ring = Ring(a2, "kT", [128, S], BF16, 2)
            qT_ring = Ring(a2, "qT", [128, S], BF16, 2)
            v_ring = Ring(a2, "vv", [128, NT, 128], BF16, 2)
            pt_ring = Ring(a2, "pt", [128, 512], BF16, 4)
            rec_ring = Ring(a2, "rec", [128, 512], F32, 2)
            ot_ring = Ring(a2, "ot", [128, 512], BF16, 2)
            if kind == "mla":
                kpe = a2("kpe_s", [128, S], BF16)
                qr = a2("qr_s", [128, S], BF16)
                P.op("sp", lambda e: e.dma_start(out=kpe[:], in_=kpe_d), w=["kpe_s"], dma=True)
                P.op("sp", lambda e: e.dma_start(out=qr[:], in_=aqr_d), w=["qr_s"], dma=True)
                scale = (128 + 64) ** -0.5
            else:
                mbT_ring = Ring(a2, "mbT", [32, S], BF16, 2)
                scale = 128 ** -0.5
            sp_pool = PsPool([0, 1, 2])
            o_pool = PsPool([3, 5])
            for j in range(2):
                kT, kTtok = kT_ring.next()
                qT, qTtok = qT_ring.next()
                vv, vtok = v_ring.next()
                ksrc = (akn_d if kind == "mla" else mk_d)[j]
                qsrc = (aqn_d if kind == "mla" else mq_d)[j]
                vsrc = (av_d if kind == "mla" else mv_d).rearrange("p (t h n) -> p t h n", h=2, n=128)[:, :, j, :]
                P.op("sp", lambda e, kT=kT, ksrc=ksrc: e.dma_start(out=kT[:], in_=ksrc), w=[kTtok], dma=True)
                P.op("sp", lambda e, qT=qT, qsrc=qsrc: e.dma_start(out=qT[:], in_=qsrc), w=[qTtok], dma=True)
                P.op("sp", lambda e, vv=vv, vsrc=vsrc: e.dma_start(out=vv[:], in_=vsrc), w=[vtok], dma=True)
                if kind == "moba":
                    mbT, mbTtok = mbT_ring.next()
                    P.op("sp", lambda e, mbT=mbT, j=j: e.dma_start(out=mbT[:], in_=mb_d[j]), w=[mbTtok], dma=True)
                slot = j if kind == "mla" else 2 + j
                for g in range(NG):
                    gc = slice(g * 512, (g + 1) * 512)
                    nk = 4 * (g + 1)
                    ob_, _ = o_pool.next()
                    oacc, sacc = banks[ob_], banks[ob_ + 1]
                    otok, stok_ = ("ps", ob_), ("ps", ob_ + 1)
                    LAG = 2
                    pts = {}
                    for step in range(nk + LAG):
                        if step < nk:
                            kt = step
                            kc = slice(kt * 128, (kt + 1) * 128)
                            sp_, sptok = sp_pool.next()
                            mm(banks[sp_][:], kT[:, kc], qT[:, gc], True, False, [kTtok, qTtok], [sptok])
                            if kind == "mla":
                                hs = slice(j * 64, (j + 1) * 64)
                                mm(banks[sp_][:], kpe[hs, kc], qr[hs, gc], False, True, ["kpe_s", "qr_s"], [sptok])
                            else:
                                n_ = kt // 2
                                mm(banks[sp_][:], oh[:, n_ * 128:(n_ + 1) * 128], mbT[:, gc], False, True, ["oh", mbTtok], [sptok])
                            pt, pttok = pt_ring.next()
                            P.op("act", lambda e, pt=pt, sp_=sp_, scale=scale: e.activation(out=pt[:], in_=banks[sp_][:], func=AF.Exp, scale=scale),
                                 r=[sptok], w=[pttok])
                            if kt >= 4 * g:
                                jj = kt - 4 * g
                                P.op("dve", lambda e, pt=pt, jj=jj: e.tensor_tensor(out=pt[:], in0=pt[:], in1=cm[:, jj, :], op=ALU.mult),
                                     r=[pttok, "cm"], w=[pttok])
                            pts[kt] = (pt, pttok)
                        if step >= LAG:
                            kt = step - LAG
                            pt, pttok = pts.pop(kt)
                            mm(oacc[:], vv[:, kt, :], pt[:], kt == 0, kt == nk - 1, [vtok, pttok], [otok])
                            mm(sacc[:], ones[:], pt[:], kt == 0, kt == nk - 1, ["ones", pttok], [stok_])
                    rec, rectok = rec_ring.next()
                    P.op("dve", lambda e, rec=rec, sacc=sacc: e.reciprocal(out=rec[:], in_=sacc[:]), r=[stok_], w=[rectok])
                    ot, ottok = ot_ring.next()
                    P.op("dve", lambda e, ot=ot, oacc=oacc, rec=rec: e.tensor_tensor(out=ot[:], in0=oacc[:], in1=rec[:], op=ALU.mult),
                         r=[otok, rectok], w=[ottok])
                    P.op("pool", lambda e, ot=ot, slot=slot, g=g: e.dma_start(
                        out=cc_in[(g * 512) // CW, slot * 128:(slot + 1) * 128, (g * 512) % CW:(g * 512) % CW + 512], in_=ot[:]),
                        r=[ottok], dma=True)
                    if debug:
                        P.op("pool", lambda e, ot=ot, slot=slot, g=g: e.dma_start(
                            out=oall_d[slot * 128:(slot + 1) * 128, g * 512:(g + 1) * 512], in_=ot[:]), r=[ottok], dma=True)
        P.barrier()

    if stop == 2:
        P.emit(final_wait=[("pool", len(P.ops["pool"]) - 1)])
        return nc
    for k_ in range(NCH):
        P.op("pool", lambda e, k_=k_: e.collective_compute("AllGather", ALU.bypass, replica_groups=[[0, 1, 2, 3], [4, 5, 6, 7]],
                                                          ins=[cc_in[k_]], outs=[cc_out[k_ + 1]]), w=[("cc_out", k_)], dma=True, inc=1)
    P.barrier()

    if stop == 3:
        P.emit(final_wait=[("pool", len(P.ops["pool"]) - 1)])
        return nc

    def own_blk(e):
        return (e.partition_id() % 4) * ((OWN // CW) * 16)

    with contextlib.ExitStack() as st3:
        def a3(name, shape, dt):
            return st3.enter_context(nc.sbuf_tensor("p3_" + name, list(shape), dt))

        gattn = a3("gattn3", [128, D], F32)
        P.op("sp", lambda e: e.dma_start(out=gattn[:], in_=gattn_d), w=["gattn3"], dma=True)
        bgate = a3("bgate", [128, 16], F32)
        P.op("sp", lambda e: e.dma_start(out=bgate[:], in_=bgate_d), w=["bgate"], dma=True)
        w_gate = a3("w_gate", [128, 8, 2048], BF16)
        w_oa = a3("w_oa", [128, 8, 1024], BF16)
        w_ob = a3("w_ob", [128, 8, 1024], BF16)
        w_out = a3("w_out", [128, 8, 1024], BF16)
        wload(w_gate, w_gate_d, "w_gate")
        wload(w_oa, w_oa_d, "w_oa")
        wload(w_ob, w_ob_d, "w_ob")
        wload(w_out, w_out_d, "w_out")
        ptp = PsPool([6, 7])
        psp = PsPool([0, 1, 2, 3, 4, 5])
        norm_T = make_norm_T(a3, ptp, 6)
        hT_ring = Ring(a3, "hT3", [128, 8, 512], BF16, 2)
        oT_ring = Ring(a3, "oT3", [128, 16, 512], BF16, 1)
        mix_ring = Ring(a3, "mix3", [128, 8, 512], BF16, 1)
        g_ring = Ring(a3, "g3", [128, 512], F32, 4)
        t_ring = Ring(a3, "t3", [128, 512], F32, 4)
        x1_ring = Ring(a3, "x1t", [128, D], F32, 2)
        cc_v = cc_out.rearrange("k (b p) n -> p (k b) n", p=128)
        groups = [(0, 2, 0, CW - 2)] + [(2 + 512 * k, 512, 16 * (1 + (512 * k) // CW), (512 * k) % CW) for k in range(OWN // 512)]
        for (c0, n, boff, coff) in groups:
            hT, hTtok = hT_ring.next()
            xts = []
            for t0 in range(0, n, 128):
                m = min(128, n - t0)
                xt, xtok, _, _ = norm_T(xown[c0 + t0:c0 + t0 + m, :], m, gattn, "gattn3", hT[:, :, t0:t0 + m], hTtok)
                xts.append((t0, m, xt, xtok))
            oT, oTtok = oT_ring.next()
            P.op("sp", lambda e, oT=oT, boff=boff, coff=coff, n=n: e.dma_start(
                out=oT[:, :, 0:n], in_=cc_v[:, bass.ds(own_blk(e) + boff, 16), coff:coff + n]), w=[oTtok], dma=True)
            mix, mixtok = mix_ring.next()
            for c in range(8):
                fc = slice(c * 128, (c + 1) * 128)
                pga, pgatok = psp.next()
                for k in range(8):
                    mm(banks[pga][:, 0:n], w_gate[:, k, c * 128:(c + 1) * 128], hT[:, k, 0:n], k == 0, k == 7, ["w_gate", hTtok], [pgatok])
                pgb, pgbtok = psp.next()
                for k in range(8):
                    mm(banks[pgb][:, 0:n], w_gate[:, k, 1024 + c * 128:1024 + (c + 1) * 128], hT[:, k, 0:n], k == 0, k == 7,
                       ["w_gate", hTtok], [pgbtok])
                pya, pyatok = psp.next()
                for h in range(8):
                    blk = (h // 2) * 4 + (h % 2)
                    mm(banks[pya][:, 0:n], w_oa[:, h, fc], oT[:, blk, 0:n], h == 0, h == 7, ["w_oa", oTtok], [pyatok])
                pyb, pybtok = psp.next()
                for h in range(8):
                    blk = (h // 2) * 4 + 2 + (h % 2)
                    mm(banks[pyb][:, 0:n], w_ob[:, h, fc], oT[:, blk, 0:n], h == 0, h == 7, ["w_ob", oTtok], [pybtok])
                ga, gatok = g_ring.next()
                gb, gbtok = g_ring.next()
                P.op("act", lambda e, ga=ga, pga=pga, c=c, n=n: e.activation(out=ga[:, 0:n], in_=banks[pga][:, 0:n], func=AF.Sigmoid,
                                                                        bias=bgate[:, c:c + 1], scale=1.0), r=[pgatok, "bgate"], w=[gatok])
                P.op("act", lambda e, gb=gb, pgb=pgb, c=c, n=n: e.activation(out=gb[:, 0:n], in_=banks[pgb][:, 0:n], func=AF.Sigmoid,
                                                                        bias=bgate[:, 8 + c:9 + c], scale=1.0), r=[pgbtok, "bgate"], w=[gbtok])
                ta, tatok = t_ring.next()
                tb, tbtok = t_ring.next()
                P.op("dve", lambda e, ta=ta, ga=ga, pya=pya, n=n: e.tensor_tensor(out=ta[:, 0:n], in0=banks[pya][:, 0:n], in1=ga[:, 0:n], op=ALU.mult),
                     r=[pyatok, gatok], w=[tatok])
                P.op("dve", lambda e, tb=tb, gb=gb, pyb=pyb, n=n: e.tensor_tensor(out=tb[:, 0:n], in0=banks[pyb][:, 0:n], in1=gb[:, 0:n], op=ALU.mult),
                     r=[pybtok, gbtok], w=[tbtok])
                P.op("pool", lambda e, mix=mix, ta=ta, tb=tb, c=c, n=n: e.tensor_tensor(out=mix[:, c, 0:n], in0=ta[:, 0:n], in1=tb[:, 0:n], op=ALU.add),
                     r=[tatok, tbtok], w=[mixtok])
            for (t0, m, xt, xtok) in xts:
                x1t, x1tok = x1_ring.next()
                for half in range(2):
                    po, potok = psp.next()
                    for c in range(8):
                        mm(banks[po][0:m, :], mix[:, c, t0:t0 + m], w_out[:, c, half * 512:(half + 1) * 512], c == 0, c == 7,
                           [mixtok, "w_out"], [potok])
                    P.op("dve", lambda e, x1t=x1t, po=po, xt=xt, m=m, half=half: e.tensor_tensor(
                        out=x1t[0:m, half * 512:(half + 1) * 512], in0=banks[po][0:m, :], in1=xt[0:m, half * 512:(half + 1) * 512], op=ALU.add),
                        r=[potok, xtok], w=[x1tok])
                P.op("pool", lambda e, x1t=x1t, m=m, r0=c0 + t0: e.dma_start(out=x1d[r0:r0 + m, :], in_=x1t[0:m, :]), r=[x1tok], dma=True)
    P.barrier()

    if stop == 4:
        P.emit(final_wait=[("pool", len(P.ops["pool"]) - 1)])
        return nc
    outs = []
    with contextlib.ExitStack() as st4:
        def a4(name, shape, dt):
            return st4.enter_context(nc.sbuf_tensor("p4_" + name, list(shape), dt))

        gffn = a4("gffn", [128, D], F32)
        gfin = a4("gfin", [128, D], F32)
        convw = a4("convw", [128, 2 * NPAIR, 3], F32)
        convb = a4("convb", [128, 2 * NPAIR], F32)
        P.op("sp", lambda e: e.dma_start(out=gffn[:], in_=gffn_d), w=["gffn"], dma=True)
        P.op("sp", lambda e: e.dma_start(out=gfin[:], in_=gfin_d), w=["gfin"], dma=True)
        P.op("sp", lambda e: e.dma_start(out=convw[:], in_=convw_d.rearrange("p (c j) -> p c j", j=3)), w=["convw"], dma=True)
        P.op("sp", lambda e: e.dma_start(out=convb[:], in_=convb_d), w=["convb"], dma=True)
        w_down = a4("w_down", [128, NPAIR, 1024], BF16)
        wload(w_down, w_down_d, "w_down")
        ptp = PsPool([6, 7])
        psp = PsPool([0, 1, 2, 3, 4, 5])
        norm_T = make_norm_T(a4, ptp, 2)
        NC_ = HALF + 2
        h2T = a4("h2T", [128, 8, NC_], BF16)
        actT = a4("actT", [128, NPAIR, HALF], BF16)
        wup_ring = Ring(a4, "wup", [128, 2, 8, 128], BF16, 3)
        U_ring = Ring(a4, "U", [128, NC_], F32, 2)
        acc_ring = Ring(a4, "acc", [128, HALF], F32, 2)
        sil_ring = Ring(a4, "sil", [128, HALF], F32, 1)
        xr_ring = Ring(a4, "xr", [128, D], F32, 2)
        y_ring = Ring(a4, "yy", [128, D], F32, 2)
        fs_ring = Ring(a4, "fs", [128, 4], F32, 4)
        segs = [(s0, min(512, NC_ - s0)) for s0 in range(0, NC_, 512)]
        for hf in range(2):
            R0 = hf * HALF
            h2tok = ("h2T", hf)
            for t0 in range(0, NC_, 128):
                m = min(128, NC_ - t0)
                norm_T(x1d[R0 + t0:R0 + t0 + m, :], m, gffn, "gffn", h2T[:, :, t0:t0 + m], "h2T")
            wups = {}

            def pref(c):
                if c < NPAIR:
                    wups[c] = wup_ring.next()
                    wload(wups[c][0], w_up_d[c], wups[c][1])
            pref(0)
            pref(1)
            for c in range(NPAIR):
                pref(c + 2)
                wup, wuptok = wups.pop(c)
                accs = []
                for ab in range(2):
                    U, Utok = U_ring.next()
                    for (s0, sn) in segs:
                        pu, putok = psp.next()
                        for k in range(8):
                            mm(banks[pu][:, 0:sn], wup[:, ab, k, :], h2T[:, k, s0:s0 + sn], k == 0, k == 7, [wuptok, "h2T"], [putok])
                        evac(U[:, s0:s0 + sn], banks[pu][:, 0:sn], r=[putok], w=[Utok], eng="act")
                    if hf == 0:
                        P.op("dve", lambda e, U=U: e.tensor_scalar(out=U[:, 0:2], in0=U[:, 0:2], scalar1=flag[:, 0:1], scalar2=None,
                                                                   op0=ALU.mult), r=[Utok, "flag"], w=[Utok])
                    ci = ab * NPAIR + c
                    acc, acctok = acc_ring.next()
                    P.op("act", lambda e, acc=acc, U=U, ci=ci: e.activation(out=acc[:], in_=U[:, 2:NC_], func=AF.Identity,
                                                                          scale=convw[:, ci, 2:3], bias=convb[:, ci:ci + 1]),
                         r=[Utok, "convw", "convb"], w=[acctok])
                    P.op("dve", lambda e, acc=acc, U=U, ci=ci: e.scalar_tensor_tensor(out=acc[:], in0=U[:, 1:NC_ - 1], scalar=convw[:, ci, 1:2],
                                                                                    in1=acc[:], op0=ALU.mult, op1=ALU.add),
                         r=[Utok, "convw", acctok], w=[acctok])
                    P.op("dve", lambda e, acc=acc, U=U, ci=ci: e.scalar_tensor_tensor(
                        out=acc[:], in0=U[:, 0:NC_ - 2], scalar=convw[:, ci, 0:1], in1=acc[:], op0=ALU.mult, op1=ALU.add),
                        r=[Utok, "convw", acctok], w=[acctok])
                    accs.append((acc, acctok))
                sil, siltok = sil_ring.next()
                P.op("act", lambda e, sil=sil, a=accs[0][0]: e.activation(out=sil[:], in_=a[:], func=AF.Silu), r=[accs[0][1]], w=[siltok])
                P.op("pool", lambda e, sil=sil, b_=accs[1][0], c=c: e.tensor_tensor(out=actT[:, c, :], in0=sil[:], in1=b_[:], op=ALU.mult),
                     r=[siltok, accs[1][1]], w=["actT"])
            for i in range(HALF // 128):
                tcs = slice(i * 128, (i + 1) * 128)
                xr, xrtok = xr_ring.next()
                rr = R0 + 2 + i * 128
                P.op("sp", lambda e, xr=xr, rr=rr: e.dma_start(out=xr[:], in_=x1d[rr:rr + 128, :]), w=[xrtok], dma=True)
                yy, yytok = y_ring.next()
                for half in range(2):
                    po, potok = psp.next()
                    for c in range(NPAIR):
                        mm(banks[po][:], actT[:, c, tcs], w_down[:, c, half * 512:(half + 1) * 512], c == 0, c == NPAIR - 1,
                           ["actT", "w_down"], [potok])
                    P.op("dve", lambda e, yy=yy, po=po, xr=xr, half=half: e.tensor_tensor(
                        out=yy[:, half * 512:(half + 1) * 512], in0=banks[po][:], in1=xr[:, half * 512:(half + 1) * 512], op=ALU.add),
                        r=[potok, xrtok], w=[yytok])
                fs, fstok = fs_ring.next()
                fj, fjtok = norm_T.jring.next()
                P.op("act", lambda e, fj=fj, yy=yy, fs=fs: e.activation(out=fj[:], in_=yy[:], func=AF.Square, accum_out=fs[:, 0:1]),
                     r=[yytok], w=[fjtok, fstok])
                P.op("dve", lambda e, fs=fs: e.tensor_scalar(out=fs[:, 1:2], in0=fs[:, 0:1], scalar1=1.0 / D, scalar2=EPS,
                                                             op0=ALU.mult, op1=ALU.add), r=[fstok], w=[fstok])
                P.op("act", lambda e, fs=fs: e.activation(out=fs[:, 2:3], in_=fs[:, 1:2], func=AF.Sqrt), r=[fstok], w=[fstok])
                P.op("dve", lambda e, fs=fs: e.reciprocal(out=fs[:, 3:4], in_=fs[:, 2:3]), r=[fstok], w=[fstok])
                P.op("dve", lambda e, yy=yy, fs=fs: e.scalar_tensor_tensor(out=yy[:], in0=yy[:], scalar=fs[:, 3:4], in1=gfin[:],
                                                                         op0=ALU.mult, op1=ALU.mult), r=[yytok, fstok, "gfin"], w=[yytok])
                ro = R0 + i * 128
                outs.append(P.op("pool", lambda e, yy=yy, ro=ro: e.dma_start(out=out_d[ro:ro + 128, :], in_=yy[:]), r=[yytok], dma=True))
    P.emit(final_wait=outs)
    return nc


def _chunk(w):
    K, N = w.shape
    return np.ascontiguousarray(w.reshape(K // 128, 128, N).transpose(1, 0, 2).reshape(128, (K // 128) * N))


def _swap_halves(w):
    n = w.shape[1] // 2
    return np.concatenate([w[:, n:], w[:, :n]], axis=1)


def _rope_tables(S, d):
    inv = (10000.0 ** (-np.arange(0, d, 2, dtype=np.float32) / np.float32(d))).astype(np.float32)
    ang = np.arange(S, dtype=np.float32)[:, None] * inv[None, :]
    cos, sin = np.cos(ang).astype(np.float32), np.sin(ang).astype(np.float32)
    cosT = np.concatenate([cos, cos], axis=1).T
    sinT = np.concatenate([-sin, sin], axis=1).T
    return np.ascontiguousarray(cosT), np.ascontiguousarray(sinT)


_CACHE = {}


def kernel(x, attn_norm, w_in, b_gate, q_norm, w_uq, kv_norm, w_ukv, w_o_mla, w_o_moba,
           w_out, ffn_norm, w_up, conv_w, conv_b, w_down, final_norm, _debug=False):
    f = lambda a: np.asarray(a, dtype=np.float32)
    x = f(x)
    B, S, _ = x.shape
    assert B == 2
    OWN = S // 4
    w_in, w_uq, w_ukv = f(w_in)[0], f(w_uq)[0], f(w_ukv)[0]
    bc = lambda v: np.ascontiguousarray(np.broadcast_to(f(v).reshape(1, -1), (128, f(v).size)))
    per_part = lambda v: np.ascontiguousarray(f(v).reshape(-1, 128).T)
    cos128, sin128 = _rope_tables(S, 128)
    cos64, sin64 = _rope_tables(S, 64)
    cos64 = np.concatenate([cos64, cos64], 0)
    sin64 = np.concatenate([sin64, sin64], 0)
    ident = np.eye(128, dtype=np.float32).astype(ml_dtypes.bfloat16)
    kk = np.arange(128)[:, None, None]
    jj = np.arange(4)[None, :, None]
    qq = np.arange(512)[None, None, :]
    cm = (qq >= 128 * jj + kk).astype(np.float32).astype(ml_dtypes.bfloat16).reshape(128, 2048)
    oh = np.zeros((32, 32, 128), np.float32)
    oh[np.arange(32), np.arange(32), :] = 1.0
    oh = oh.astype(ml_dtypes.bfloat16).reshape(32, 4096)
    OFF_MOBA, OFF_GATE = 704, 704 + 3072
    kpe = w_in[:, 640:704]
    w_kpe = np.concatenate([kpe, kpe, _swap_halves(kpe), _swap_halves(kpe)], 1)
    wu = f(w_up)[0]
    cw, cb = f(conv_w)[0], f(conv_b)[0]
    w_up_r = np.empty((NPAIR, 128, 2, 8, 128), np.float32)
    for c in range(NPAIR):
        for ab in range(2):
            blk = wu[:, ab * DFF + c * 128: ab * DFF + (c + 1) * 128]
            w_up_r[c, :, ab] = blk.reshape(8, 128, 128).transpose(1, 0, 2)
    w_up_r = w_up_r.reshape(NPAIR, 128, 2048)
    cw_r = cw.T.reshape(2, NPAIR, 128, 3).transpose(2, 0, 1, 3).reshape(128, 2 * NPAIR * 3)
    cb_r = cb.reshape(2, NPAIR, 128).transpose(2, 0, 1).reshape(128, 2 * NPAIR)
    common = dict(
        gattn=bc(attn_norm), glat=np.concatenate([bc(q_norm), bc(kv_norm)], 1), gffn=bc(ffn_norm), gfin=bc(final_norm),
        w_lat=_chunk(w_in[:, 0:640]), w_kpe=_chunk(w_kpe), w_gate=_chunk(w_in[:, OFF_GATE:]),
        bgate=per_part(b_gate), w_oa=_chunk(f(w_o_mla)[0]), w_ob=_chunk(f(w_o_moba)[0]), w_out=_chunk(f(w_out)[0]),
        w_up=np.ascontiguousarray(w_up_r), convw=np.ascontiguousarray(cw_r), convb=np.ascontiguousarray(cb_r),
        w_down=_chunk(f(w_down)[0]), cos128=cos128, sin128=sin128, cos64=cos64, sin64=sin64,
        ident=ident, cm=cm, oh=oh)
    in_maps = []
    for c in range(8):
        b, hp = c // 4, c % 4
        m = dict(common)
        m["xb"] = np.ascontiguousarray(x[b])
        xo = np.zeros((OWN + 2, D), np.float32)
        lo = hp * OWN - 2
        if lo >= 0:
            xo[:] = x[b, lo:lo + OWN + 2]
        else:
            xo[2:] = x[b, 0:OWN]
        m["xown"] = xo
        m["flag"] = np.full((128, 1), 0.0 if hp == 0 else 1.0, np.float32)
        hs = [2 * hp, 2 * hp + 1]
        mcol = lambda which, h: w_in[:, OFF_MOBA + which * 1024 + h * 128: OFF_MOBA + which * 1024 + (h + 1) * 128]
        m["w_mq"] = _chunk(np.concatenate([mcol(0, hs[0]), mcol(0, hs[1]), _swap_halves(mcol(0, hs[0])), _swap_halves(mcol(0, hs[1]))], 1))
        m["w_mk"] = _chunk(np.concatenate([mcol(1, hs[0]), mcol(1, hs[1]), _swap_halves(mcol(1, hs[0])), _swap_halves(mcol(1, hs[1]))], 1))
        m["w_mv"] = _chunk(np.concatenate([mcol(2, hs[0]), mcol(2, hs[1])], 1))
        qn = lambda h: w_uq[:, h * 192: h * 192 + 128]
        qr_ = lambda h: w_uq[:, h * 192 + 128: (h + 1) * 192]
        m["w_uqn"] = _chunk(np.concatenate([qn(hs[0]), qn(hs[1])], 1))
        m["w_uqr"] = _chunk(np.concatenate([qr_(hs[0]), qr_(hs[1]), _swap_halves(qr_(hs[0])), _swap_halves(qr_(hs[1]))], 1))
        m["w_ukk"] = _chunk(np.concatenate([w_ukv[:, h * 256: h * 256 + 128] for h in hs], 1))
        m["w_ukv"] = _chunk(np.concatenate([w_ukv[:, h * 256 + 128: (h + 1) * 256] for h in hs], 1))
        in_maps.append(m)
    key = (S, _debug)
    if key not in _CACHE:
        _CACHE[key] = build(S, debug=_debug)
    nc = _CACHE[key]
    res = run_bass_kernel_spmd(nc, in_maps, core_ids=list(range(8)))
    out = np.empty((B, S, D), np.float32)
    for c in range(8):
        b, hp = c // 4, c % 4
        out[b, hp * OWN:(hp + 1) * OWN] = res.results[c]["out"]
    if _debug:
        return out, res
    return out
```

```python
import contextlib
import numpy as np
import ml_dtypes
import concourse.bass as bass
import concourse.mybir as mybir
from concourse.bass_utils import run_bass_kernel_spmd

F32 = mybir.dt.float32
BF16 = mybir.dt.bfloat16
AF = mybir.ActivationFunctionType
ALU = mybir.AluOpType
AX = mybir.AxisListType

D = 1024
EPS = 1e-6
DFF = 2816
NPAIR = DFF // 128
MBIG = 30000.0


ENGS = ("pe", "act", "dve", "pool", "sp")
N_DMA_SEMS = 24


class Prog:
    def __init__(self, nc, same_engine_sync=True):
        self.nc = nc
        self.ops = {e: [] for e in ENGS}
        self.last_w = {}
        self.readers = {}
        self.same_engine_sync = same_engine_sync
        self.n_dma = 0
        self.dmas_since_bar = []
        self.bar_deps = {}

    def op(self, eng, fn, r=(), w=(), dma=False, inc=16):
        idx = len(self.ops[eng])
        deps = {}

        def add(d):
            if d is None:
                return
            e, i = d
            if e == eng and not dma and self.ops[e][i]["dma"] is None:
                if eng == "pe" or not self.same_engine_sync:
                    return
            if e == eng and i == idx:
                return
            key = (e, i) if self.ops[e][i]["dma"] is not None else e
            if key == e:
                deps[e] = max(deps.get(e, -1), i)
            else:
                deps[key] = i

        for t in r:
            add(self.last_w.get(t))
        for t in w:
            add(self.last_w.get(t))
            for rd in self.readers.get(t, ()):
                add(rd)
        for (e2, i2) in self.bar_deps.pop(eng, ()):
            if self.ops[e2][i2]["dma"] is not None:
                deps[(e2, i2)] = i2
            elif e2 != eng or eng != "pe":
                deps[e2] = max(deps.get(e2, -1), i2)
        rec = dict(fn=fn, deps=deps, dma=(self.n_dma if dma else None), sig=False, inc=inc)
        if dma:
            self.n_dma += 1
        self.ops[eng].append(rec)
        me = (eng, idx)
        if dma:
            self.dmas_since_bar.append(me)
        for t in r:
            self.readers.setdefault(t, []).append(me)
        for t in w:
            self.last_w[t] = me
            self.readers[t] = []
        return me

    def barrier(self):
        deps = list(self.dmas_since_bar)
        for e in ENGS:
            for i in range(len(self.ops[e]) - 1, -1, -1):
                if self.ops[e][i]["dma"] is None:
                    deps.append((e, i))
                    break
        for e in ENGS:
            self.bar_deps[e] = list(self.bar_deps.get(e, [])) + deps
        self.dmas_since_bar = []

    def emit(self, final_wait=()):
        nc = self.nc
        for e in ENGS:
            for rec in self.ops[e]:
                for k, i in rec["deps"].items():
                    pe = k if isinstance(k, str) else k[0]
                    self.ops[pe][i]["sig"] = True
        for (e, i) in final_wait:
            self.ops[e][i]["sig"] = True
        import contextlib
        stack = contextlib.ExitStack()
        sems = {e: stack.enter_context(nc.semaphore("s_" + e)) for e in ENGS}
        dsems = [stack.enter_context(nc.semaphore("d%d" % i)) for i in range(N_DMA_SEMS)]
        dcount = [0] * N_DMA_SEMS
        for e in ENGS:
            c = 0
            for rec in self.ops[e]:
                if rec["dma"] is not None:
                    continue
                if rec["sig"]:
                    c += 1
                    rec["val"] = c
        dma_recs = []
        for e in ENGS:
            for rec in self.ops[e]:
                if rec["dma"] is not None:
                    dma_recs.append(rec)
        dma_recs.sort(key=lambda r: r["dma"])
        for rec in dma_recs:
            s = rec["dma"] % N_DMA_SEMS
            rec["dsem"] = s
            rec["prev"] = dcount[s]
            dcount[s] += rec["inc"]
            rec["val"] = dcount[s]

        handles = {"pe": nc.tensor, "act": nc.scalar, "dve": nc.vector, "pool": nc.gpsimd, "sp": nc.sync}
        block = stack.enter_context(nc.Block())

        def run(e, h):
            known = {}
            knownd = {}
            for rec in self.ops[e]:
                for k, i in rec["deps"].items():
                    if isinstance(k, str):
                        v = self.ops[k][i]["val"]
                        if known.get(k, 0) >= v:
                            continue
                        known[k] = v
                        h.wait_ge(sems[k], v)
                    else:
                        d = self.ops[k[0]][i]
                        if knownd.get(d["dsem"], 0) < d["val"]:
                            knownd[d["dsem"]] = d["val"]
                            h.wait_ge(dsems[d["dsem"]], d["val"])
                if rec["dma"] is not None and rec["prev"] > knownd.get(rec["dsem"], 0):
                    knownd[rec["dsem"]] = rec["prev"]
                    h.wait_ge(dsems[rec["dsem"]], rec["prev"])
                ins = rec["fn"](h)
                if rec["dma"] is not None:
                    ins.then_inc(dsems[rec["dsem"]], rec["inc"])
                elif rec["sig"]:
                    ins.then_inc(sems[e], 1)
            if e == "sp":
                for (fe, fi) in final_wait:
                    d = self.ops[fe][fi]
                    if d["dma"] is not None:
                        h.wait_ge(dsems[d["dsem"]], d["val"])
                    else:
                        h.wait_ge(sems[fe], d["val"])

        block.tensor(lambda h: run("pe", h))
        block.scalar(lambda h: run("act", h))
        block.vector(lambda h: run("dve", h))
        block.gpsimd(lambda h: run("pool", h))
        block.sync(lambda h: run("sp", h))
        stack.close()


class Ring:
    def __init__(self, alloc, name, shape, dtype, n):
        self.t = [alloc(f"{name}{i}", shape, dtype) for i in range(n)]
        self.i = 0
        self.name = name
        self.n = n

    def next(self):
        k = self.i % self.n
        self.i += 1
        return self.t[k], (self.name, k)


def build(S, debug=False, stop=0):
    import os
    stop = int(os.environ.get("KSTOP", stop))
    NG = S // 512
    NT = S // 128
    OWN = S // 4
    HALF = OWN // 2
    nc = bass.Bass("TRN2", target_bir_lowering=False)
    P = Prog(nc)

    def din(name, shape, dt=F32):
        return nc.dram_tensor(name, list(shape), dt, kind="ExternalInput").ap()

    def dscr(name, shape, dt=BF16):
        if debug:
            return nc.dram_tensor(name, list(shape), dt, kind="ExternalOutput").ap()
        return nc.dram_tensor(name, list(shape), dt).ap()

    xb = din("xb", [S, D])
    xown = din("xown", [OWN + 2, D])
    flag_d = din("flag", [128, 1])
    gattn_d = din("gattn", [128, D])
    glat_d = din("glat", [128, 640])
    gffn_d = din("gffn", [128, D])
    gfin_d = din("gfin", [128, D])
    w_lat_d = din("w_lat", [128, 8 * 640])
    w_kpe_d = din("w_kpe", [128, 8 * 256])
    w_mq_d = din("w_mq", [128, 8 * 512])
    w_mk_d = din("w_mk", [128, 8 * 512])
    w_mv_d = din("w_mv", [128, 8 * 256])
    w_uqn_d = din("w_uqn", [128, 3 * 256])
    w_uqr_d = din("w_uqr", [128, 3 * 256])
    w_ukk_d = din("w_ukk", [128, 2 * 256])
    w_ukv_d = din("w_ukv", [128, 2 * 256])
    w_gate_d = din("w_gate", [128, 8 * 2048])
    bgate_d = din("bgate", [128, 16])
    w_oa_d = din("w_oa", [128, 8 * 1024])
    w_ob_d = din("w_ob", [128, 8 * 1024])
    w_out_d = din("w_out", [128, 8 * 1024])
    w_up_d = din("w_up", [NPAIR, 128, 2 * 8 * 128])
    convw_d = din("convw", [128, 2 * NPAIR * 3])
    convb_d = din("convb", [128, 2 * NPAIR])
    w_down_d = din("w_down", [128, NPAIR * 1024])
    cos128_d = din("cos128", [128, S])
    sin128_d = din("sin128", [128, S])
    cos64_d = din("cos64", [128, S])
    sin64_d = din("sin64", [128, S])
    ident_d = din("ident", [128, 128], BF16)
    cm_d = din("cm", [128, 4 * 512], BF16)
    oh_d = din("oh", [32, 32 * 128], BF16)
    out_d = nc.dram_tensor("out", [OWN, D], F32, kind="ExternalOutput").ap()

    aqn_d = dscr("aqn", [2, 128, S])
    aqr_d = dscr("aqr", [128, S])
    akn_d = dscr("akn", [2, 128, S])
    kpe_d = dscr("kpe", [128, S])
    av_d = dscr("av", [128, NT * 2 * 128])
    mq_d = dscr("mq", [2, 128, S])
    mk_d = dscr("mk", [2, 128, S])
    mv_d = dscr("mv", [128, NT * 2 * 128])
    mb_d = dscr("mb", [2, 32, S])
    CW = min(1024, OWN)
    NCH = S // CW
    cc_in = nc.dram_tensor("cc_in", [NCH, 512, CW], BF16).ap()
    cc_out = nc.dram_tensor("cc_out", [NCH + 1, 2048, CW], BF16).ap()
    oall_d = dscr("oall", [512, S]) if debug else None
    x1d = dscr("x1d", [OWN + 2, D], F32)

    def alloc(name, shape, dt):
        return nc.alloc_sbuf_tensor(name, list(shape), dt)

    ident = alloc("ident_s", [128, 128], BF16)
    ones = alloc("ones_s", [128, 128], BF16)
    cm = alloc("cm_s", [128, 4, 512], BF16)
    oh = alloc("oh_s", [32, 32 * 128], BF16)
    PB = alloc("PB_s", [128, 64], F32)
    OW = alloc("OW_s", [128, 64], F32)
    flag = alloc("flag_s", [128, 1], F32)
    zpad = alloc("zpad_s", [128, 16, 2], BF16)
    P.op("sp", lambda e: e.dma_start(out=ident[:], in_=ident_d), w=["ident"], dma=True)
    P.op("sp", lambda e: e.dma_start(out=cm[:], in_=cm_d.rearrange("p (j q) -> p j q", j=4)), w=["cm"], dma=True)
    P.op("sp", lambda e: e.dma_start(out=oh[:], in_=oh_d), w=["oh"], dma=True)
    P.op("sp", lambda e: e.dma_start(out=flag[:], in_=flag_d), w=["flag"], dma=True)
    P.op("pool", lambda e: e.memset(ones[:], 1.0), w=["ones"])
    P.op("pool", lambda e: e.memset(PB[:, 0:32], 0.0), w=["PB"])
    P.op("pool", lambda e: e.memset(PB[:, 32:64], -1e30), w=["PB"])
    P.op("pool", lambda e: e.memset(OW[:], 0.0), w=["OW"])
    P.op("pool", lambda e: e.memset(OW[:, 32:33], 1.0), w=["OW"])
    P.op("pool", lambda e: e.memset(zpad[:], 0.0), w=["zpad"])
    P.op("pool", lambda e: e.dma_start(out=cc_out[0, :, CW - 2:CW].rearrange("(b p) n -> p b n", p=128), in_=zpad[:]),
         r=["zpad"], dma=True)

    banks = [nc.alloc_psum_tensor(f"ps{i}", [128, 512], F32) for i in range(8)]
    banks_bf = [b.bitcast(BF16) for b in banks]

    class PsPool:
        def __init__(self, idxs):
            self.idxs = idxs
            self.i = 0

        def next(self):
            k = self.idxs[self.i % len(self.idxs)]
            self.i += 1
            return k, ("ps", k)

    ev_i = [0]

    def evac(out_ap, in_ap, r, w, eng=None):
        if eng is None:
            eng = ("act", "dve")[ev_i[0] % 2]
            ev_i[0] += 1
        if eng == "act":
            P.op("act", lambda e: e.copy(out=out_ap, in_=in_ap), r=r, w=w)
        else:
            P.op("dve", lambda e: e.tensor_copy(out=out_ap, in_=in_ap), r=r, w=w)

    def mm(out_ap, lhsT, rhs, start, stop, r, w):
        P.op("pe", lambda e: e.matmul(out_ap, lhsT=lhsT, rhs=rhs, start=start, stop=stop), r=r, w=w)

    def wload(dst, src2d, tok):
        shp = list(dst.shape)
        if len(shp) == 3:
            src = src2d.rearrange("p (c n) -> p c n", c=shp[1])
        elif len(shp) == 4:
            src = src2d.rearrange("p (a c n) -> p a c n", a=shp[1], c=shp[2])
        else:
            src = src2d
        P.op("pool", lambda e: e.dma_start(out=dst[:], in_=src), w=[tok], dma=True)

    def make_norm_T(alloc_f, pT_pool, nx):
        xring = Ring(alloc_f, "xt", [128, D], F32, nx)
        jring = Ring(alloc_f, "junk", [128, D], BF16, 2)
        hring = Ring(alloc_f, "hrow", [128, D], BF16, 2)
        sring = Ring(alloc_f, "ssn", [128, 4], F32, 4)

        def norm_T(src_rows, m, gbc, gtok, dst3, dst_tok):
            xt, xtok = xring.next()
            P.op("sp", lambda e: e.dma_start(out=xt[0:m, :], in_=src_rows), w=[xtok], dma=True)
            jk, jtok = jring.next()
            ss, stok = sring.next()
            P.op("act", lambda e: e.activation(out=jk[0:m, :], in_=xt[0:m, :], func=AF.Square, accum_out=ss[0:m, 0:1]),
                 r=[xtok], w=[jtok, stok])
            P.op("dve", lambda e: e.tensor_scalar(out=ss[0:m, 1:2], in0=ss[0:m, 0:1], scalar1=1.0 / D, scalar2=EPS,
                                                  op0=ALU.mult, op1=ALU.add), r=[stok], w=[stok])
            P.op("act", lambda e: e.activation(out=ss[0:m, 2:3], in_=ss[0:m, 1:2], func=AF.Sqrt), r=[stok], w=[stok])
            P.op("dve", lambda e: e.reciprocal(out=ss[0:m, 3:4], in_=ss[0:m, 2:3]), r=[stok], w=[stok])
            hr, htok = hring.next()
            P.op("dve", lambda e: e.scalar_tensor_tensor(out=hr[0:m, :], in0=xt[0:m, :], scalar=ss[0:m, 3:4],
                                                         in1=gbc[0:m, :], op0=ALU.mult, op1=ALU.mult),
                 r=[xtok, stok, gtok], w=[htok])
            k, ptok = pT_pool.next()
            pT = banks_bf[k]
            for c in range(8):
                P.op("pe", lambda e, c=c: e.transpose(out=pT[:, c * 128:c * 128 + m], in_=hr[0:m, c * 128:(c + 1) * 128],
                                                      identity=ident[0:m, 0:m]), r=[htok, "ident"], w=[ptok])
            evac(dst3, pT[:, :].rearrange("p (c t) -> p c t", c=8)[:, :, 0:m], r=[ptok], w=[dst_tok])
            return xt, xtok, ss, stok

        norm_T.jring = jring
        return norm_T

    with contextlib.ExitStack() as st1:
        def a1(name, shape, dt):
            return st1.enter_context(nc.sbuf_tensor("p1_" + name, list(shape), dt))

        gattn = a1("gattn", [128, D], F32)
        glat = a1("glat", [128, 640], F32)
        P.op("sp", lambda e: e.dma_start(out=gattn[:], in_=gattn_d), w=["gattn"], dma=True)
        P.op("sp", lambda e: e.dma_start(out=glat[:], in_=glat_d), w=["glat"], dma=True)
        w_lat = a1("w_lat", [128, 8, 640], BF16)
        w_kpe = a1("w_kpe", [128, 8, 256], BF16)
        w_mq = a1("w_mq", [128, 8, 512], BF16)
        w_mk = a1("w_mk", [128, 8, 512], BF16)
        w_mv = a1("w_mv", [128, 8, 256], BF16)
        w_uqn = a1("w_uqn", [128, 3, 256], BF16)
        w_uqr = a1("w_uqr", [128, 3, 256], BF16)
        w_ukk = a1("w_ukk", [128, 2, 256], BF16)
        w_ukv = a1("w_ukv", [128, 2, 256], BF16)
        for t_, d_, nm in ((w_lat, w_lat_d, "w_lat"), (w_kpe, w_kpe_d, "w_kpe"), (w_mk, w_mk_d, "w_mk"),
                           (w_mq, w_mq_d, "w_mq"), (w_mv, w_mv_d, "w_mv"), (w_uqn, w_uqn_d, "w_uqn"),
                           (w_uqr, w_uqr_d, "w_uqr"), (w_ukk, w_ukk_d, "w_ukk"), (w_ukv, w_ukv_d, "w_ukv")):
            wload(t_, d_, nm)
        ksum = a1("ksum", [128, 2, 32], F32)
        P.op("pool", lambda e: e.memset(ksum[:], 0.0), w=["ksum0", "ksum1"])

        ptp = PsPool([6, 7])
        psp = PsPool([0, 1, 2, 3, 4, 5])
        norm_T = make_norm_T(a1, ptp, 4)
        hT_ring = Ring(a1, "hT", [128, 8, 512], BF16, 2)
        latn_ring = Ring(a1, "latn", [128, 640], BF16, 6)
        latT_ring = Ring(a1, "latT", [128, 5, 512], BF16, 2)
        ss2_ring = Ring(a1, "ss2", [128, 8], F32, 8)
        tab_ring = Ring(a1, "tab", [128, 4, 512], F32, 2)
        t1_ring = Ring(a1, "rt1", [128, 512], F32, 2)
        t2_ring = Ring(a1, "rt2", [128, 512], F32, 2)
        of_ring = Ring(a1, "rof", [128, 512], F32, 2)
        ofq_ring = Ring(a1, "rofq", [128, 512], F32, 4)
        ob_ring = Ring(a1, "rob", [128, 512], BF16, 4)
        vst_ring = Ring(a1, "vst", [128, 4, 256], BF16, 2)
        gm_ring = Ring(a1, "gm", [128, 48], F32, 8)
        mbt_ring = Ring(a1, "mbt", [128, 4, 32], BF16, 4)
        mbs_ring = Ring(a1, "mbs", [32, 512], BF16, 2)
        junk2 = Ring(a1, "junk2", [128, 384], BF16, 2)

        def rope(pa, patok, pb, pbtok, tab, ttok, ci, si, out_ap, out_tok):
            t1, t1tok = t1_ring.next()
            t2, t2tok = t2_ring.next()
            P.op("dve", lambda e: e.tensor_tensor(out=t1[:], in0=banks[pa][:], in1=tab[:, ci, :], op=ALU.mult),
                 r=[patok, ttok], w=[t1tok])
            P.op("dve", lambda e: e.tensor_tensor(out=t2[:], in0=banks[pb][:], in1=tab[:, si, :], op=ALU.mult),
                 r=[pbtok, ttok], w=[t2tok])
            P.op("pool", lambda e: e.tensor_tensor(out=out_ap, in0=t1[:], in1=t2[:], op=ALU.add),
                 r=[t1tok, t2tok], w=[out_tok])

        def proj_pair(wt, wtok, ca, cb, rhsT, rtok, nk, koff=0):
            pa, patok = psp.next()
            for k in range(nk):
                mm(banks[pa][:], wt[:, k, ca:ca + 128], rhsT[:, koff + k, :], k == 0, k == nk - 1, [wtok, rtok], [patok])
            pb, pbtok = psp.next()
            for k in range(nk):
                mm(banks[pb][:], wt[:, k, cb:cb + 128], rhsT[:, koff + k, :], k == 0, k == nk - 1, [wtok, rtok], [pbtok])
            return pa, patok, pb, pbtok

        def proj_one(wt, wtok, ca, rhsT, rtok, nk, koff=0):
            pa, patok = psp.next()
            for k in range(nk):
                mm(banks[pa][:], wt[:, k, ca:ca + 128], rhsT[:, koff + k, :], k == 0, k == nk - 1, [wtok, rtok], [patok])
            return pa, patok

        def store(dst, src, r):
            P.op("pool", lambda e: e.dma_start(out=dst, in_=src), r=r, dma=True)

        gst = {}

        def stage_A(g):
            gc = slice(g * 512, (g + 1) * 512)
            hT, hTtok = hT_ring.next()
            tab, ttok = tab_ring.next()
            gst[g] = dict(hT=hT, hTtok=hTtok, tab=tab, ttok=ttok, gc=gc)
            for i_, td in enumerate((cos128_d, sin128_d, cos64_d, sin64_d)):
                P.op("sp", lambda e, i_=i_, td=td, tab=tab, gc=gc: e.dma_start(out=tab[:, i_, :], in_=td[:, gc]), w=[ttok], dma=True)
            for t in range(4):
                r0 = g * 512 + t * 128
                norm_T(xb[r0:r0 + 128, :], 128, gattn, "gattn", hT[:, :, t * 128:(t + 1) * 128], hTtok)

        def stage_B(g):
            G = gst[g]
            hT, hTtok = G["hT"], G["hTtok"]
            G["ln"] = []
            for t in range(4):
                tc_ = slice(t * 128, (t + 1) * 128)
                pq, pqtok = psp.next()
                for k in range(8):
                    mm(banks[pq][:, 0:384], hT[:, k, tc_], w_lat[:, k, 0:384], k == 0, k == 7, [hTtok, "w_lat"], [pqtok])
                pk, pktok = psp.next()
                for k in range(8):
                    mm(banks[pk][:, 0:256], hT[:, k, tc_], w_lat[:, k, 384:640], k == 0, k == 7, [hTtok, "w_lat"], [pktok])
                ss, stok = ss2_ring.next()
                j2, j2tok = junk2.next()
                P.op("act", lambda e, pq=pq, j2=j2, ss=ss: e.activation(out=j2[:, 0:384], in_=banks[pq][:, 0:384], func=AF.Square,
                                                                     accum_out=ss[:, 0:1]), r=[pqtok], w=[j2tok, stok])
                P.op("act", lambda e, pk=pk, j2=j2, ss=ss: e.activation(out=j2[:, 0:256], in_=banks[pk][:, 0:256], func=AF.Square,
                                                                     accum_out=ss[:, 1:2]), r=[pktok], w=[j2tok, stok])
                P.op("dve", lambda e, ss=ss: e.tensor_scalar(out=ss[:, 2:3], in0=ss[:, 0:1], scalar1=1.0 / 384, scalar2=EPS,
                                                             op0=ALU.mult, op1=ALU.add), r=[stok], w=[stok])
                P.op("dve", lambda e, ss=ss: e.tensor_scalar(out=ss[:, 3:4], in0=ss[:, 1:2], scalar1=1.0 / 256, scalar2=EPS,
                                                             op0=ALU.mult, op1=ALU.add), r=[stok], w=[stok])
                P.op("act", lambda e, ss=ss: e.activation(out=ss[:, 4:6], in_=ss[:, 2:4], func=AF.Sqrt), r=[stok], w=[stok])
                P.op("dve", lambda e, ss=ss: e.reciprocal(out=ss[:, 6:8], in_=ss[:, 4:6]), r=[stok], w=[stok])
                ln, lntok = latn_ring.next()
                P.op("dve", lambda e, pq=pq, ln=ln, ss=ss: e.scalar_tensor_tensor(
                    out=ln[:, 0:384], in0=banks[pq][:, 0:384], scalar=ss[:, 6:7], in1=glat[:, 0:384],
                    op0=ALU.mult, op1=ALU.mult), r=[pqtok, stok, "glat"], w=[lntok])
                P.op("dve", lambda e, pk=pk, ln=ln, ss=ss: e.scalar_tensor_tensor(
                    out=ln[:, 384:640], in0=banks[pk][:, 0:256], scalar=ss[:, 7:8], in1=glat[:, 384:640],
                    op0=ALU.mult, op1=ALU.mult), r=[pktok, stok, "glat"], w=[lntok])
                G["ln"].append((ln, lntok))

        def stage_C(g):
            G = gst[g]
            hT, hTtok, tab, ttok, gc = G["hT"], G["hTtok"], G["tab"], G["ttok"], G["gc"]
            for j in range(2):
                pa, patok, pb, pbtok = proj_pair(w_mk, "w_mk", j * 128, 256 + j * 128, hT, hTtok, 8)
                of, oftok = of_ring.next()
                rope(pa, patok, pb, pbtok, tab, ttok, 0, 1, of[:], oftok)
                ob, obtok = ob_ring.next()
                P.op("act", lambda e, ob=ob, of=of: e.copy(out=ob[:], in_=of[:]), r=[oftok], w=[obtok])
                store(mk_d[j, :, gc], ob[:], [obtok])
                P.op("dve", lambda e, of=of, j=j, g=g: e.tensor_reduce(out=ksum[:, j, 2 * g:2 * g + 2],
                                                                   in_=of[:].rearrange("p (a b) -> p a b", a=2),
                                                                   axis=AX.X, op=ALU.add), r=[oftok], w=[f"ksum{j}"])
            qofs = []
            for j in range(2):
                pa, patok, pb, pbtok = proj_pair(w_mq, "w_mq", j * 128, 256 + j * 128, hT, hTtok, 8)
                of, oftok = ofq_ring.next()
                rope(pa, patok, pb, pbtok, tab, ttok, 0, 1, of[:], oftok)
                ob, obtok = ob_ring.next()
                P.op("act", lambda e, ob=ob, of=of: e.copy(out=ob[:], in_=of[:]), r=[oftok], w=[obtok])
                store(mq_d[j, :, gc], ob[:], [obtok])
                qofs.append((of, oftok))
            vst, vsttok = vst_ring.next()
            for t in range(4):
                tc_ = slice(t * 128, (t + 1) * 128)
                pv, pvtok = psp.next()
                for k in range(8):
                    mm(banks[pv][:, 0:256], hT[:, k, tc_], w_mv[:, k, :], k == 0, k == 7, [hTtok, "w_mv"], [pvtok])
                evac(vst[:, t, :], banks[pv][:, 0:256], r=[pvtok], w=[vsttok])
            store(mv_d[:, g * 1024:(g + 1) * 1024].rearrange("p (t n) -> p t n", t=4), vst[:], [vsttok])
            pa, patok, pb, pbtok = proj_pair(w_kpe, "w_kpe", 0, 128, hT, hTtok, 8)
            ob, obtok = ob_ring.next()
            rope(pa, patok, pb, pbtok, tab, ttok, 2, 3, ob[:], obtok)
            store(kpe_d[:, gc], ob[:], [obtok])
            G["mbt"] = []
            for j in range(2):
                of, oftok = qofs[j]
                mbt, mbttok = mbt_ring.next()
                for t in range(4):
                    B = 2 * g + t // 2
                    pg, pgtok = psp.next()
                    mm(banks[pg][:, 0:32], of[:, t * 128:(t + 1) * 128], ksum[:, j, :], True, True, [oftok, f"ksum{j}"], [pgtok])
                    gm, gmtok = gm_ring.next()
                    P.op("dve", lambda e, pg=pg, gm=gm, B=B: e.tensor_tensor(out=gm[:, 0:32], in0=banks[pg][:, 0:32],
                                                                          in1=PB[:, 32 - B:64 - B], op=ALU.add),
                         r=[pgtok, "PB"], w=[gmtok])
                    P.op("dve", lambda e, gm=gm: e.max(out=gm[:, 32:40], in_=gm[:, 0:32]), r=[gmtok], w=[gmtok])
                    P.op("dve", lambda e, gm=gm: e.tensor_scalar(out=gm[:, 40:41], in0=gm[:, 34:35], scalar1=-1e29, scalar2=None,
                                                                 op0=ALU.max), r=[gmtok], w=[gmtok])
                    P.op("dve", lambda e, gm=gm: e.tensor_scalar(out=gm[:, 0:32], in0=gm[:, 0:32], scalar1=gm[:, 40:41],
                                                                 scalar2=None, op0=ALU.is_ge), r=[gmtok], w=[gmtok])
                    P.op("dve", lambda e, gm=gm, B=B: e.tensor_tensor(out=gm[:, 0:32], in0=gm[:, 0:32], in1=OW[:, 32 - B:64 - B],
                                                                      op=ALU.add), r=[gmtok, "OW"], w=[gmtok])
                    P.op("dve", lambda e, gm=gm, mbt=mbt, t=t: e.tensor_scalar(out=mbt[:, t, :], in0=gm[:, 0:32], scalar1=MBIG,
                                                                             scalar2=-MBIG, op0=ALU.mult, op1=ALU.add),
                         r=[gmtok], w=[mbttok])
                G["mbt"].append((mbt, mbttok))

        def stage_D(g):
            G = gst[g]
            tab, ttok, gc = G["tab"], G["ttok"], G["gc"]
            latT, latTtok = latT_ring.next()
            for t in range(4):
                tc_ = slice(t * 128, (t + 1) * 128)
                ln, lntok = G["ln"][t]
                kk, ptok = ptp.next()
                pT = banks_bf[kk]
                for c in range(5):
                    P.op("pe", lambda e, c=c, pT=pT, ln=ln: e.transpose(out=pT[:, c * 128:(c + 1) * 128],
                                                                        in_=ln[:, c * 128:(c + 1) * 128], identity=ident[:]),
                         r=[lntok, "ident"], w=[ptok])
                evac(latT[:, :, tc_], pT[:, 0:640].rearrange("p (c t) -> p c t", c=5), r=[ptok], w=[latTtok])
            for j in range(2):
                pa, patok = proj_one(w_uqn, "w_uqn", j * 128, latT, latTtok, 3)
                ob, obtok = ob_ring.next()
                evac(ob[:], banks[pa][:], r=[patok], w=[obtok])
                store(aqn_d[j, :, gc], ob[:], [obtok])
            pa, patok, pb, pbtok = proj_pair(w_uqr, "w_uqr", 0, 128, latT, latTtok, 3)
            ob, obtok = ob_ring.next()
            rope(pa, patok, pb, pbtok, tab, ttok, 2, 3, ob[:], obtok)
            store(aqr_d[:, gc], ob[:], [obtok])
            for j in range(2):
                pa, patok = proj_one(w_ukk, "w_ukk", j * 128, latT, latTtok, 2, koff=3)
                ob, obtok = ob_ring.next()
                evac(ob[:], banks[pa][:], r=[patok], w=[obtok])
                store(akn_d[j, :, gc], ob[:], [obtok])
            vst, vsttok = vst_ring.next()
            for t in range(4):
                tc_ = slice(t * 128, (t + 1) * 128)
                pv, pvtok = psp.next()
                for k in range(2):
                    mm(banks[pv][:, 0:256], latT[:, 3 + k, tc_], w_ukv[:, k, :], k == 0, k == 1, [latTtok, "w_ukv"], [pvtok])
                evac(vst[:, t, :], banks[pv][:, 0:256], r=[pvtok], w=[vsttok])
            store(av_d[:, g * 1024:(g + 1) * 1024].rearrange("p (t n) -> p t n", t=4), vst[:], [vsttok])

        def stage_E(g):
            G = gst[g]
            gc = G["gc"]
            for j in range(2):
                mbt, mbttok = G["mbt"][j]
                kk, ptok = ptp.next()
                pTb = banks_bf[kk]
                for t in range(4):
                    P.op("pe", lambda e, pTb=pTb, mbt=mbt, t=t: e.transpose(out=pTb[0:32, t * 128:(t + 1) * 128], in_=mbt[:, t, :],
                                                                           identity=ident[:]), r=[mbttok, "ident"], w=[ptok])
                mbs, mbstok = mbs_ring.next()
                evac(mbs[:], pTb[0:32, 0:512], r=[ptok], w=[mbstok])
                store(mb_d[j, :, gc], mbs[:], [mbstok])
            del gst[g]

        stage_A(0)
        for g in range(NG):
            stage_B(g)
            stage_C(g)
            if g + 1 < NG:
                stage_A(g + 1)
            stage_D(g)
            stage_E(g)
    P.barrier()
    if stop == 1:
        P.emit(final_wait=[("pool", len(P.ops["pool"]) - 1)])
        return nc

    for kind in ("mla", "moba"):
        with contextlib.ExitStack() as st2:
            def a2(name, shape, dt):
                return st2.enter_context(nc.sbuf_tensor("p2" + kind + "_" + name, list(shape), dt))

            kT_ring = Ring(a2, "kT", [128, S], BF16, 2)
            qT_ring = Ring(a2, "qT", [128, S], BF16, 2)
            v_ring = Ring(a2, "vv", [128, NT, 128], BF16, 2)
            pt_ring = Ring(a2, "pt", [128, 512], BF16, 4)
            rec_ring = Ring(a2, "rec", [128, 512], F32, 2)
            ot_ring = Ring(a2, "ot", [128, 512], BF16, 2)
            if kind == "mla":
                kpe = a2("kpe_s", [128, S], BF16)
                qr = a2("qr_s", [128, S], BF16)
                P.op("sp", lambda e: e.dma_start(out=kpe[:], in_=kpe_d), w=["kpe_s"], dma=True)
                P.op("sp", lambda e: e.dma_start(out=qr[:], in_=aqr_d), w=["qr_s"], dma=True)
                scale = (128 + 64) ** -0.5
            else:
                mbT_ring = Ring(a2, "mbT", [32, S], BF16, 2)
                scale = 128 ** -0.5
            sp_pool = PsPool([0, 1, 2])
            o_pool = PsPool([3, 5])
            for j in range(2):
                kT, kTtok = kT_ring.next()
                qT, qTtok = qT_ring.next()
                vv, vtok = v_ring.next()
                ksrc = (akn_d if kind == "mla" else mk_d)[j]
                qsrc = (aqn_d if kind == "mla" else mq_d)[j]
                vsrc = (av_d if kind == "mla" else mv_d).rearrange("p (t h n) -> p t h n", h=2, n=128)[:, :, j, :]
                P.op("sp", lambda e, kT=kT, ksrc=ksrc: e.dma_start(out=kT[:], in_=ksrc), w=[kTtok], dma=True)
                P.op("sp", lambda e, qT=qT, qsrc=qsrc: e.dma_start(out=qT[:], in_=qsrc), w=[qTtok], dma=True)
                P.op("sp", lambda e, vv=vv, vsrc=vsrc: e.dma_start(out=vv[:], in_=vsrc), w=[vtok], dma=True)
                if kind == "moba":
                    mbT, mbTtok = mbT_ring.next()
                    P.op("sp", lambda e, mbT=mbT, j=j: e.dma_start(out=mbT[:], in_=mb_d[j]), w=[mbTtok], dma=True)
                slot = j if kind == "mla" else 2 + j
                for g in range(NG):
                    gc = slice(g * 512, (g + 1) * 512)
                    nk = 4 * (g + 1)
                    ob_, _ = o_pool.next()
                    oacc, sacc = banks[ob_], banks[ob_ + 1]
                    otok, stok_ = ("ps", ob_), ("ps", ob_ + 1)
                    LAG = 2
                    pts = {}
                    for step in range(nk + LAG):
                        if step < nk:
                            kt = step
                            kc = slice(kt * 128, (kt + 1) * 128)
                            sp_, sptok = sp_pool.next()
                            mm(banks[sp_][:], kT[:, kc], qT[:, gc], True, False, [kTtok, qTtok], [sptok])
                            if kind == "mla":
                                hs = slice(j * 64, (j + 1) * 64)
                                mm(banks[sp_][:], kpe[hs, kc], qr[hs, gc], False, True, ["kpe_s", "qr_s"], [sptok])
                            else:
                                n_ = kt // 2
                                mm(banks[sp_][:], oh[:, n_ * 128:(n_ + 1) * 128], mbT[:, gc], False, True, ["oh", mbTtok], [sptok])
                            pt, pttok = pt_ring.next()
                            P.op("act", lambda e, pt=pt, sp_=sp_, scale=scale: e.activation(out=pt[:], in_=banks[sp_][:], func=AF.Exp, scale=scale),
                                 r=[sptok], w=[pttok])
                            if kt >= 4 * g:
                                jj = kt - 4 * g
                                P.op("dve", lambda e, pt=pt, jj=jj: e.tensor_tensor(out=pt[:], in0=pt[:], in1=cm[:, jj, :], op=ALU.mult),
                                     r=[pttok, "cm"], w=[pttok])
                            pts[kt] = (pt, pttok)
                        if step >= LAG:
                            kt = step - LAG
                            pt, pttok = pts.pop(kt)
                            mm(oacc[:], vv[:, kt, :], pt[:], kt == 0, kt == nk - 1, [vtok, pttok], [otok])
                            mm(sacc[:], ones[:], pt[:], kt == 0, kt == nk - 1, ["ones", pttok], [stok_])
                    rec, rectok = rec_ring.next()
                    P.op("dve", lambda e, rec=rec, sacc=sacc: e.reciprocal(out=rec[:], in_=sacc[:]), r=[stok_], w=[rectok])
                    ot, ottok = ot_ring.next()
                    P.op("dve", lambda e, ot=ot, oacc=oacc, rec=rec: e.tensor_tensor(out=ot[:], in0=oacc[:], in1=rec[:], op=ALU.mult),
                         r=[otok, rectok], w=[ottok])
                    P.op("pool", lambda e, ot=ot, slot=slot, g=g: e.dma_start(
                        out=cc_in[(g * 512) // CW, slot * 128:(slot + 1) * 128, (g * 512) % CW:(g * 512) % CW + 512], in_=ot[:]),
                        r=[ottok], dma=True)
                    if debug:
                        P.op("pool", lambda e, ot=ot, slot=slot, g=g: e.dma_start(
                            out=oall_d[slot * 128:(slot + 1) * 128, g * 512:(g + 1) * 512], in_=ot[:]), r=[ottok], dma=True)
        P.barrier()

    if stop == 2:
        P.emit(final_wait=[("pool", len(P.ops["pool"]) - 1)])
        return nc
    for k_ in range(NCH):
        P.op("pool", lambda e, k_=k_: e.collective_compute("AllGather", ALU.bypass, replica_groups=[[0, 1, 2, 3], [4, 5, 6, 7]],
                                                          ins=[cc_in[k_]], outs=[cc_out[k_ + 1]]), w=[("cc_out", k_)], dma=True, inc=1)
    P.barrier()

    if stop == 3:
        P.emit(final_wait=[("pool", len(P.ops["pool"]) - 1)])
        return nc

    def own_blk(e):
        return (e.partition_id() % 4) * ((OWN // CW) * 16)

    with contextlib.ExitStack() as st3:
        def a3(name, shape, dt):
            return st3.enter_context(nc.sbuf_tensor("p3_" + name, list(shape), dt))

        gattn = a3("gattn3", [128, D], F32)
        P.op("sp", lambda e: e.dma_start(out=gattn[:], in_=gattn_d), w=["gattn3"], dma=True)
        bgate = a3("bgate", [128, 16], F32)
        P.op("sp", lambda e: e.dma_start(out=bgate[:], in_=bgate_d), w=["bgate"], dma=True)
        w_gate = a3("w_gate", [128, 8, 2048], BF16)
        w_oa = a3("w_oa", [128, 8, 1024], BF16)
        w_ob = a3("w_ob", [128, 8, 1024], BF16)
        w_out = a3("w_out", [128, 8, 1024], BF16)
        wload(w_gate, w_gate_d, "w_gate")
        wload(w_oa, w_oa_d, "w_oa")
        wload(w_ob, w_ob_d, "w_ob")
        wload(w_out, w_out_d, "w_out")
        ptp = PsPool([6, 7])
        psp = PsPool([0, 1, 2, 3, 4, 5])
        norm_T = make_norm_T(a3, ptp, 6)
        hT_ring = Ring(a3, "hT3", [128, 8, 512], BF16, 2)
        oT_ring = Ring(a3, "oT3", [128, 16, 512], BF16, 1)
        mix_ring = Ring(a3, "mix3", [128, 8, 512], BF16, 1)
        g_ring = Ring(a3, "g3", [128, 512], F32, 4)
        t_ring = Ring(a3, "t3", [128, 512], F32, 4)
        x1_ring = Ring(a3, "x1t", [128, D], F32, 2)
        cc_v = cc_out.rearrange("k (b p) n -> p (k b) n", p=128)
        groups = [(0, 2, 0, CW - 2)] + [(2 + 512 * k, 512, 16 * (1 + (512 * k) // CW), (512 * k) % CW) for k in range(OWN // 512)]
        for (c0, n, boff, coff) in groups:
            hT, hTtok = hT_ring.next()
            xts = []
            for t0 in range(0, n, 128):
                m = min(128, n - t0)
                xt, xtok, _, _ = norm_T(xown[c0 + t0:c0 + t0 + m, :], m, gattn, "gattn3", hT[:, :, t0:t0 + m], hTtok)
                xts.append((t0, m, xt, xtok))
            oT, oTtok = oT_ring.next()
            P.op("sp", lambda e, oT=oT, boff=boff, coff=coff, n=n: e.dma_start(
                out=oT[:, :, 0:n], in_=cc_v[:, bass.ds(own_blk(e) + boff, 16), coff:coff + n]), w=[oTtok], dma=True)
            mix, mixtok = mix_ring.next()
            for c in range(8):
                fc = slice(c * 128, (c + 1) * 128)
                pga, pgatok = psp.next()
                for k in range(8):
                    mm(banks[pga][:, 0:n], w_gate[:, k, c * 128:(c + 1) * 128], hT[:, k, 0:n], k == 0, k == 7, ["w_gate", hTtok], [pgatok])
                pgb, pgbtok = psp.next()
                for k in range(8):
                    mm(banks[pgb][:, 0:n], w_gate[:, k, 1024 + c * 128:1024 + (c + 1) * 128], hT[:, k, 0:n], k == 0, k == 7,
                       ["w_gate", hTtok], [pgbtok])
                pya, pyatok = psp.next()
                for h in range(8):
                    blk = (h // 2) * 4 + (h % 2)
                    mm(banks[pya][:, 0:n], w_oa[:, h, fc], oT[:, blk, 0:n], h == 0, h == 7, ["w_oa", oTtok], [pyatok])
                pyb, pybtok = psp.next()
                for h in range(8):
                    blk = (h // 2) * 4 + 2 + (h % 2)
                    mm(banks[pyb][:, 0:n], w_ob[:, h, fc], oT[:, blk, 0:n], h == 0, h == 7, ["w_ob", oTtok], [pybtok])
                ga, gatok = g_ring.next()
                gb, gbtok = g_ring.next()
                P.op("act", lambda e, ga=ga, pga=pga, c=c, n=n: e.activation(out=ga[:, 0:n], in_=banks[pga][:, 0:n], func=AF.Sigmoid,
                                                                        bias=bgate[:, c:c + 1], scale=1.0), r=[pgatok, "bgate"], w=[gatok])
                P.op("act", lambda e, gb=gb, pgb=pgb, c=c, n=n: e.activation(out=gb[:, 0:n], in_=banks[pgb][:, 0:n], func=AF.Sigmoid,
                                                                        bias=bgate[:, 8 + c:9 + c], scale=1.0), r=[pgbtok, "bgate"], w=[gbtok])
                ta, tatok = t_ring.next()
                tb, tbtok = t_ring.next()
                P.op("dve", lambda e, ta=ta, ga=ga, pya=pya, n=n: e.tensor_tensor(out=ta[:, 0:n], in0=banks[pya][:, 0:n], in1=ga[:, 0:n], op=ALU.mult),
                     r=[pyatok, gatok], w=[tatok])
                P.op("dve", lambda e, tb=tb, gb=gb, pyb=pyb, n=n: e.tensor_tensor(out=tb[:, 0:n], in0=banks[pyb][:, 0:n], in1=gb[:, 0:n], op=ALU.mult),
                     r=[pybtok, gbtok], w=[tbtok])
                P.op("pool", lambda e, mix=mix, ta=ta, tb=tb, c=c, n=n: e.tensor_tensor(out=mix[:, c, 0:n], in0=ta[:, 0:n], in1=tb[:, 0:n], op=ALU.add),
                     r=[tatok, tbtok], w=[mixtok])
            for (t0, m, xt, xtok) in xts:
                x1t, x1tok = x1_ring.next()
                for half in range(2):
                    po, potok = psp.next()
                    for c in range(8):
                        mm(banks[po][0:m, :], mix[:, c, t0:t0 + m], w_out[:, c, half * 512:(half + 1) * 512], c == 0, c == 7,
                           [mixtok, "w_out"], [potok])
                    P.op("dve", lambda e, x1t=x1t, po=po, xt=xt, m=m, half=half: e.tensor_tensor(
                        out=x1t[0:m, half * 512:(half + 1) * 512], in0=banks[po][0:m, :], in1=xt[0:m, half * 512:(half + 1) * 512], op=ALU.add),
                        r=[potok, xtok], w=[x1tok])
                P.op("pool", lambda e, x1t=x1t, m=m, r0=c0 + t0: e.dma_start(out=x1d[r0:r0 + m, :], in_=x1t[0:m, :]), r=[x1tok], dma=True)
    P.barrier()

    if stop == 4:
        P.emit(final_wait=[("pool", len(P.ops["pool"]) - 1)])
        return nc
    outs = []
    with contextlib.ExitStack() as st4:
        def a4(name, shape, dt):
            return st4.enter_context(nc.sbuf_tensor("p4_" + name, list(shape), dt))

        gffn = a4("gffn", [128, D], F32)
        gfin = a4("gfin", [128, D], F32)
        convw = a4("convw", [128, 2 * NPAIR, 3], F32)
        convb = a4("convb", [128, 2 * NPAIR], F32)
        P.op("sp", lambda e: e.dma_start(out=gffn[:], in_=gffn_d), w=["gffn"], dma=True)
        P.op("sp", lambda e: e.dma_start(out=gfin[:], in_=gfin_d), w=["gfin"], dma=True)
        P.op("sp", lambda e: e.dma_start(out=convw[:], in_=convw_d.rearrange("p (c j) -> p c j", j=3)), w=["convw"], dma=True)
        P.op("sp", lambda e: e.dma_start(out=convb[:], in_=convb_d), w=["convb"], dma=True)
        w_down = a4("w_down", [128, NPAIR, 1024], BF16)
        wload(w_down, w_down_d, "w_down")
        ptp = PsPool([6, 7])
        psp = PsPool([0, 1, 2, 3, 4, 5])
        norm_T = make_norm_T(a4, ptp, 2)
        NC_ = HALF + 2
        h2T = a4("h2T", [128, 8, NC_], BF16)
        actT = a4("actT", [128, NPAIR, HALF], BF16)
        wup_ring = Ring(a4, "wup", [128, 2, 8, 128], BF16, 3)
        U_ring = Ring(a4, "U", [128, NC_], F32, 2)
        acc_ring = Ring(a4, "acc", [128, HALF], F32, 2)
        sil_ring = Ring(a4, "sil", [128, HALF], F32, 1)
        xr_ring = Ring(a4, "xr", [128, D], F32, 2)
        y_ring = Ring(a4, "yy", [128, D], F32, 2)
        fs_ring = Ring(a4, "fs", [128, 4], F32, 4)
        segs = [(s0, min(512, NC_ - s0)) for s0 in range(0, NC_, 512)]
        for hf in range(2):
            R0 = hf * HALF
            h2tok = ("h2T", hf)
            for t0 in range(0, NC_, 128):
                m = min(128, NC_ - t0)
                norm_T(x1d[R0 + t0:R0 + t0 + m, :], m, gffn, "gffn", h2T[:, :, t0:t0 + m], "h2T")
            wups = {}

            def pref(c):
                if c < NPAIR:
                    wups[c] = wup_ring.next()
                    wload(wups[c][0], w_up_d[c], wups[c][1])
            pref(0)
            pref(1)
            for c in range(NPAIR):
                pref(c + 2)
                wup, wuptok = wups.pop(c)
                accs = []
                for ab in range(2):
                    U, Utok = U_ring.next()
                    for (s0, sn) in segs:
                        pu, putok = psp.next()
                        for k in range(8):
                            mm(banks[pu][:, 0:sn], wup[:, ab, k, :], h2T[:, k, s0:s0 + sn], k == 0, k == 7, [wuptok, "h2T"], [putok])
                        evac(U[:, s0:s0 + sn], banks[pu][:, 0:sn], r=[putok], w=[Utok], eng="act")
                    if hf == 0:
                        P.op("dve", lambda e, U=U: e.tensor_scalar(out=U[:, 0:2], in0=U[:, 0:2], scalar1=flag[:, 0:1], scalar2=None,
                                                                   op0=ALU.mult), r=[Utok, "flag"], w=[Utok])
                    ci = ab * NPAIR + c
                    acc, acctok = acc_ring.next()
                    P.op("act", lambda e, acc=acc, U=U, ci=ci: e.activation(out=acc[:], in_=U[:, 2:NC_], func=AF.Identity,
                                                                          scale=convw[:, ci, 2:3], bias=convb[:, ci:ci + 1]),
                         r=[Utok, "convw", "convb"], w=[acctok])
                    P.op("dve", lambda e, acc=acc, U=U, ci=ci: e.scalar_tensor_tensor(out=acc[:], in0=U[:, 1:NC_ - 1], scalar=convw[:, ci, 1:2],
                                                                                    in1=acc[:], op0=ALU.mult, op1=ALU.add),
                         r=[Utok, "convw", acctok], w=[acctok])
                    P.op("dve", lambda e, acc=acc, U=U, ci=ci: e.scalar_tensor_tensor(
                        out=acc[:], in0=U[:, 0:NC_ - 2], scalar=convw[:, ci, 0:1], in1=acc[:], op0=ALU.mult, op1=ALU.add),
                        r=[Utok, "convw", acctok], w=[acctok])
                    accs.append((acc, acctok))
                sil, siltok = sil_ring.next()
                P.op("act", lambda e, sil=sil, a=accs[0][0]: e.activation(out=sil[:], in_=a[:], func=AF.Silu), r=[accs[0][1]], w=[siltok])
                P.op("pool", lambda e, sil=sil, b_=accs[1][0], c=c: e.tensor_tensor(out=actT[:, c, :], in0=sil[:], in1=b_[:], op=ALU.mult),
                     r=[siltok, accs[1][1]], w=["actT"])
            for i in range(HALF // 128):
                tcs = slice(i * 128, (i + 1) * 128)
                xr, xrtok = xr_ring.next()
                rr = R0 + 2 + i * 128
                P.op("sp", lambda e, xr=xr, rr=rr: e.dma_start(out=xr[:], in_=x1d[rr:rr + 128, :]), w=[xrtok], dma=True)
                yy, yytok = y_ring.next()
                for half in range(2):
                    po, potok = psp.next()
                    for c in range(NPAIR):
                        mm(banks[po][:], actT[:, c, tcs], w_down[:, c, half * 512:(half + 1) * 512], c == 0, c == NPAIR - 1,
                           ["actT", "w_down"], [potok])
                    P.op("dve", lambda e, yy=yy, po=po, xr=xr, half=half: e.tensor_tensor(
                        out=yy[:, half * 512:(half + 1) * 512], in0=banks[po][:], in1=xr[:, half * 512:(half + 1) * 512], op=ALU.add),
                        r=[potok, xrtok], w=[yytok])
                fs, fstok = fs_ring.next()
                fj, fjtok = norm_T.jring.next()
                P.op("act", lambda e, fj=fj, yy=yy, fs=fs: e.activation(out=fj[:], in_=yy[:], func=AF.Square, accum_out=fs[:, 0:1]),
                     r=[yytok], w=[fjtok, fstok])
                P.op("dve", lambda e, fs=fs: e.tensor_scalar(out=fs[:, 1:2], in0=fs[:, 0:1], scalar1=1.0 / D, scalar2=EPS,
                                                             op0=ALU.mult, op1=ALU.add), r=[fstok], w=[fstok])
                P.op("act", lambda e, fs=fs: e.activation(out=fs[:, 2:3], in_=fs[:, 1:2], func=AF.Sqrt), r=[fstok], w=[fstok])
                P.op("dve", lambda e, fs=fs: e.reciprocal(out=fs[:, 3:4], in_=fs[:, 2:3]), r=[fstok], w=[fstok])
                P.op("dve", lambda e, yy=yy, fs=fs: e.scalar_tensor_tensor(out=yy[:], in0=yy[:], scalar=fs[:, 3:4], in1=gfin[:],
                                                                         op0=ALU.mult, op1=ALU.mult), r=[yytok, fstok, "gfin"], w=[yytok])
                ro = R0 + i * 128
                outs.append(P.op("pool", lambda e, yy=yy, ro=ro: e.dma_start(out=out_d[ro:ro + 128, :], in_=yy[:]), r=[yytok], dma=True))
    P.emit(final_wait=outs)
    return nc


def _chunk(w):
    K, N = w.shape
    return np.ascontiguousarray(w.reshape(K // 128, 128, N).transpose(1, 0, 2).reshape(128, (K // 128) * N))


def _swap_halves(w):
    n = w.shape[1] // 2
    return np.concatenate([w[:, n:], w[:, :n]], axis=1)


def _rope_tables(S, d):
    inv = (10000.0 ** (-np.arange(0, d, 2, dtype=np.float32) / np.float32(d))).astype(np.float32)
    ang = np.arange(S, dtype=np.float32)[:, None] * inv[None, :]
    cos, sin = np.cos(ang).astype(np.float32), np.sin(ang).astype(np.float32)
    cosT = np.concatenate([cos, cos], axis=1).T
    sinT = np.concatenate([-sin, sin], axis=1).T
    return np.ascontiguousarray(cosT), np.ascontiguousarray(sinT)


_CACHE = {}


def kernel(x, attn_norm, w_in, b_gate, q_norm, w_uq, kv_norm, w_ukv, w_o_mla, w_o_moba,
           w_out, ffn_norm, w_up, conv_w, conv_b, w_down, final_norm, _debug=False):
    f = lambda a: np.asarray(a, dtype=np.float32)
    x = f(x)
    B, S, _ = x.shape
    assert B == 2
    OWN = S // 4
    w_in, w_uq, w_ukv = f(w_in)[0], f(w_uq)[0], f(w_ukv)[0]
    bc = lambda v: np.ascontiguousarray(np.broadcast_to(f(v).reshape(1, -1), (128, f(v).size)))
    per_part = lambda v: np.ascontiguousarray(f(v).reshape(-1, 128).T)
    cos128, sin128 = _rope_tables(S, 128)
    cos64, sin64 = _rope_tables(S, 64)
    cos64 = np.concatenate([cos64, cos64], 0)
    sin64 = np.concatenate([sin64, sin64], 0)
    ident = np.eye(128, dtype=np.float32).astype(ml_dtypes.bfloat16)
    kk = np.arange(128)[:, None, None]
    jj = np.arange(4)[None, :, None]
    qq = np.arange(512)[None, None, :]
    cm = (qq >= 128 * jj + kk).astype(np.float32).astype(ml_dtypes.bfloat16).reshape(128, 2048)
    oh = np.zeros((32, 32, 128), np.float32)
    oh[np.arange(32), np.arange(32), :] = 1.0
    oh = oh.astype(ml_dtypes.bfloat16).reshape(32, 4096)
    OFF_MOBA, OFF_GATE = 704, 704 + 3072
    kpe = w_in[:, 640:704]
    w_kpe = np.concatenate([kpe, kpe, _swap_halves(kpe), _swap_halves(kpe)], 1)
    wu = f(w_up)[0]
    cw, cb = f(conv_w)[0], f(conv_b)[0]
    w_up_r = np.empty((NPAIR, 128, 2, 8, 128), np.float32)
    for c in range(NPAIR):
        for ab in range(2):
            blk = wu[:, ab * DFF + c * 128: ab * DFF + (c + 1) * 128]
            w_up_r[c, :, ab] = blk.reshape(8, 128, 128).transpose(1, 0, 2)
    w_up_r = w_up_r.reshape(NPAIR, 128, 2048)
    cw_r = cw.T.reshape(2, NPAIR, 128, 3).transpose(2, 0, 1, 3).reshape(128, 2 * NPAIR * 3)
    cb_r = cb.reshape(2, NPAIR, 128).transpose(2, 0, 1).reshape(128, 2 * NPAIR)
    common = dict(
        gattn=bc(attn_norm), glat=np.concatenate([bc(q_norm), bc(kv_norm)], 1), gffn=bc(ffn_norm), gfin=bc(final_norm),
        w_lat=_chunk(w_in[:, 0:640]), w_kpe=_chunk(w_kpe), w_gate=_chunk(w_in[:, OFF_GATE:]),
        bgate=per_part(b_gate), w_oa=_chunk(f(w_o_mla)[0]), w_ob=_chunk(f(w_o_moba)[0]), w_out=_chunk(f(w_out)[0]),
        w_up=np.ascontiguousarray(w_up_r), convw=np.ascontiguousarray(cw_r), convb=np.ascontiguousarray(cb_r),
        w_down=_chunk(f(w_down)[0]), cos128=cos128, sin128=sin128, cos64=cos64, sin64=sin64,
        ident=ident, cm=cm, oh=oh)
    in_maps = []
    for c in range(8):
        b, hp = c // 4, c % 4
        m = dict(common)
        m["xb"] = np.ascontiguousarray(x[b])
        xo = np.zeros((OWN + 2, D), np.float32)
        lo = hp * OWN - 2
        if lo >= 0:
            xo[:] = x[b, lo:lo + OWN + 2]
        else:
            xo[2:] = x[b, 0:OWN]
        m["xown"] = xo
        m["flag"] = np.full((128, 1), 0.0 if hp == 0 else 1.0, np.float32)
        hs = [2 * hp, 2 * hp + 1]
        mcol = lambda which, h: w_in[:, OFF_MOBA + which * 1024 + h * 128: OFF_MOBA + which * 1024 + (h + 1) * 128]
        m["w_mq"] = _chunk(np.concatenate([mcol(0, hs[0]), mcol(0, hs[1]), _swap_halves(mcol(0, hs[0])), _swap_halves(mcol(0, hs[1]))], 1))
        m["w_mk"] = _chunk(np.concatenate([mcol(1, hs[0]), mcol(1, hs[1]), _swap_halves(mcol(1, hs[0])), _swap_halves(mcol(1, hs[1]))], 1))
        m["w_mv"] = _chunk(np.concatenate([mcol(2, hs[0]), mcol(2, hs[1])], 1))
        qn = lambda h: w_uq[:, h * 192: h * 192 + 128]
        qr_ = lambda h: w_uq[:, h * 192 + 128: (h + 1) * 192]
        m["w_uqn"] = _chunk(np.concatenate([qn(hs[0]), qn(hs[1])], 1))
        m["w_uqr"] = _chunk(np.concatenate([qr_(hs[0]), qr_(hs[1]), _swap_halves(qr_(hs[0])), _swap_halves(qr_(hs[1]))], 1))
        m["w_ukk"] = _chunk(np.concatenate([w_ukv[:, h * 256: h * 256 + 128] for h in hs], 1))
        m["w_ukv"] = _chunk(np.concatenate([w_ukv[:, h * 256 + 128: (h + 1) * 256] for h in hs], 1))
        in_maps.append(m)
    key = (S, _debug)
    if key not in _CACHE:
        _CACHE[key] = build(S, debug=_debug)
    nc = _CACHE[key]
    res = run_bass_kernel_spmd(nc, in_maps, core_ids=list(range(8)))
    out = np.empty((B, S, D), np.float32)
    for c in range(8):
        b, hp = c // 4, c % 4
        out[b, hp * OWN:(hp + 1) * OWN] = res.results[c]["out"]
    if _debug:
        return out, res
    return out
```
